# Optimizing a Trainium2 kernel written in Bass

```python
import jax
import jax.numpy as jnp
from jax import lax
import numpy as np

D_MODEL = 2048
BATCH = 8
SEQ = 2048
DEPTH = 4

GRID_W = 64
CTX_LEN = 256
F32 = jnp.float32
EPS = 1e-6
ROPE_BASE = 10000.0
N_MOD = 6
N_EVEN = (DEPTH + 1) // 2
N_ODD = DEPTH // 2

FOURIER_GROUPS = 4
FOURIER_GROUP_DIM = D_MODEL // 16
FOURIER_DIM = FOURIER_GROUPS * FOURIER_GROUP_DIM
ATTN_HEAD_DIM = 128
ATTN_HEADS = (D_MODEL - FOURIER_DIM) // ATTN_HEAD_DIM
ATTN_KV_HEADS = 4
ATTN_GROUP = ATTN_HEADS // ATTN_KV_HEADS
ATTN_Q_BLOCK = 128
EVEN_IN_DIM = FOURIER_DIM + (ATTN_HEADS + 2 * ATTN_KV_HEADS) * ATTN_HEAD_DIM
EVEN_MIX_DIM = FOURIER_DIM + ATTN_HEADS * ATTN_HEAD_DIM

RET_HEADS = 8
RET_QK_DIM = D_MODEL // RET_HEADS
RET_V_DIM = 2 * RET_QK_DIM
RET_CHUNK = 128
ODD_IN_DIM = 2 * RET_HEADS * RET_QK_DIM + 2 * RET_HEADS * RET_V_DIM
ODD_MIX_DIM = RET_HEADS * RET_V_DIM

FFN_DIM = -(-8 * D_MODEL // 768) * 256

kernel_name = 'hybrid_fourier_gqa_retention_dit'


def rms_norm(x, eps=EPS):
    xf = x.astype(F32)
    return (xf * lax.rsqrt(jnp.mean(xf * xf, axis=-1, keepdims=True) + eps)).astype(x.dtype)


def modulate(x, shift, scale):
    return rms_norm(x) * (1.0 + scale) + shift


def grid_positions(n_tokens):
    rows = n_tokens // GRID_W
    row = jnp.repeat(jnp.arange(rows, dtype=jnp.int32), GRID_W)
    col = jnp.tile(jnp.arange(GRID_W, dtype=jnp.int32), rows)
    return row, col


def rope_1d(x, pos):
    dp = x.shape[-1]
    inv = ROPE_BASE ** (-jnp.arange(0, dp, 2, dtype=F32) / dp)
    ang = pos.astype(F32)[:, None] * inv[None, :]
    cos = jnp.cos(ang)[None, :, None, :]
    sin = jnp.sin(ang)[None, :, None, :]
    x1, x2 = jnp.split(x.astype(F32), 2, axis=-1)
    return jnp.concatenate([x1 * cos - x2 * sin, x2 * cos + x1 * sin], axis=-1).astype(x.dtype)


def axial_rope(x, row, col):
    half = x.shape[-1] // 2
    return jnp.concatenate([rope_1d(x[..., :half], row), rope_1d(x[..., half:], col)], axis=-1)


def head_rms_norm(x, gain):
    return rms_norm(x) * gain


def fourier_mix(u):
    b, t, _ = u.shape
    uf = u.astype(F32).reshape(b, t, FOURIER_GROUPS, FOURIER_GROUP_DIM)
    y = jnp.fft.fftn(uf, axes=(1, 3), norm='ortho').real
    return y.reshape(b, t, FOURIER_DIM).astype(u.dtype)


def softmax_attend(q, k, v):
    s = jnp.einsum('bqkgd,bskd->bkgqs', q.astype(F32), k.astype(F32)) * (ATTN_HEAD_DIM ** -0.5)
    p = jax.nn.softmax(s, axis=-1)
    return jnp.einsum('bkgqs,bskd->bqkgd', p, v.astype(F32)).astype(q.dtype)


def blocked_attend(q, k, v):
    b, t = q.shape[:2]
    nb = t // ATTN_Q_BLOCK
    qb = jnp.moveaxis(q.reshape(b, nb, ATTN_Q_BLOCK, *q.shape[2:]), 1, 0)
    ob = lax.map(lambda blk: softmax_attend(blk, k, v), qb)
    return jnp.moveaxis(ob, 0, 1).reshape(q.shape)


def split_even(p):
    b, t, _ = p.shape
    hq = ATTN_HEADS * ATTN_HEAD_DIM
    hkv = ATTN_KV_HEADS * ATTN_HEAD_DIM
    f, q, k, v = jnp.split(p, [FOURIER_DIM, FOURIER_DIM + hq, FOURIER_DIM + hq + hkv], axis=-1)
    return (f, q.reshape(b, t, ATTN_HEADS, ATTN_HEAD_DIM),
            k.reshape(b, t, ATTN_KV_HEADS, ATTN_HEAD_DIM),
            v.reshape(b, t, ATTN_KV_HEADS, ATTN_HEAD_DIM))


def group_heads(q):
    b, t = q.shape[:2]
    return q.reshape(b, t, ATTN_KV_HEADS, ATTN_GROUP, ATTN_HEAD_DIM)


def fourier_gqa_mixer(u_ctx, u_lat, w_in, w_out, q_gain, k_gain, need_ctx):
    b, n_lat, _ = u_lat.shape
    row, col = grid_positions(n_lat)
    f_c, q_c, k_c, v_c = split_even(u_ctx @ w_in)
    f_l, q_l, k_l, v_l = split_even(u_lat @ w_in)
    k_c = head_rms_norm(k_c, k_gain)
    k_l = axial_rope(head_rms_norm(k_l, k_gain), row, col)
    q_l = axial_rope(head_rms_norm(q_l, q_gain), row, col)
    k_all = jnp.concatenate([k_l, k_c], axis=1)
    v_all = jnp.concatenate([v_l, v_c], axis=1)
    a_l = blocked_attend(group_heads(q_l), k_all, v_all).reshape(b, n_lat, -1)
    o_l = jnp.concatenate([fourier_mix(f_l), a_l], axis=-1) @ w_out
    o_c = None
    if need_ctx:
        q_c = head_rms_norm(q_c, q_gain)
        a_c = softmax_attend(group_heads(q_c), k_c, v_c).reshape(b, u_ctx.shape[1], -1)
        o_c = jnp.concatenate([fourier_mix(f_c), a_c], axis=-1) @ w_out
    return o_c, o_l


def retention_scan(q, k, v, log_gamma, init_state, strict):
    b, h, t, _ = q.shape
    dv = v.shape[-1]
    n = t // RET_CHUNK
    idx = jnp.arange(RET_CHUNK, dtype=F32)
    diff = idx[:, None] - idx[None, :]
    lg = log_gamma.astype(F32)[:, None, None]
    mask = diff > 0 if strict else diff >= 0
    inner_decay = jnp.where(mask, jnp.exp(jnp.maximum(diff, 0.0) * lg), 0.0)
    q_decay = jnp.exp((idx + 1.0) * lg[:, :, 0])[..., None]
    k_decay = jnp.exp((RET_CHUNK - 1.0 - idx) * lg[:, :, 0])[..., None]
    chunk_decay = jnp.exp(RET_CHUNK * lg)

    def to_chunks(a):
        return jnp.moveaxis(a.reshape(b, h, n, RET_CHUNK, a.shape[-1]), 2, 0)

    def step(state, inp):
        qc, kc, vc = inp
        scores = jnp.einsum('bhid,bhjd->bhij', qc, kc) * inner_decay
        out = (jnp.einsum('bhij,bhje->bhie', scores, vc)
               + jnp.einsum('bhid,bhde->bhie', qc * q_decay, state))
        state = state * chunk_decay + jnp.einsum('bhjd,bhje->bhde', kc * k_decay, vc)
        return state, out

    final, out = lax.scan(step, init_state, (to_chunks(q), to_chunks(k), to_chunks(v)))
    return jnp.moveaxis(out, 0, 2).reshape(b, h, t, dv), final


def bidir_retention(q, k, v, lg_fwd, lg_bwd, init_fwd, init_bwd):
    o_f, s_f = retention_scan(q, k, v, lg_fwd, init_fwd, False)
    flip = lambda a: jnp.flip(a, axis=2)
    o_b, s_b = retention_scan(flip(q), flip(k), flip(v), lg_bwd, init_bwd, True)
    return o_f + flip(o_b), s_f, s_b


def split_odd(p):
    b, t, _ = p.shape
    dqk = RET_HEADS * RET_QK_DIM
    dvv = RET_HEADS * RET_V_DIM
    q, k, v, g = jnp.split(p, [dqk, 2 * dqk, 2 * dqk + dvv], axis=-1)
    return (q.reshape(b, t, RET_HEADS, RET_QK_DIM), k.reshape(b, t, RET_HEADS, RET_QK_DIM),
            v.reshape(b, t, RET_HEADS, RET_V_DIM), g)


def retention_mixer(u_ctx, u_lat, w_in, w_out, lg_fwd, lg_bwd, need_ctx):
    b, n_lat, _ = u_lat.shape
    row, col = grid_positions(n_lat)
    q_c, k_c, v_c, g_c = split_odd(u_ctx @ w_in)
    q_l, k_l, v_l, g_l = split_odd(u_lat @ w_in)
    q_l = axial_rope(q_l, row, col)
    k_l = axial_rope(k_l, row, col)
    bhtd = lambda a: jnp.swapaxes(a, 1, 2).astype(F32)
    k_scale = RET_QK_DIM ** -0.5
    zero = jnp.zeros((b, RET_HEADS, RET_QK_DIM, RET_V_DIM), F32)
    o_c, s_f, s_b = bidir_retention(bhtd(q_c), bhtd(k_c) * k_scale, bhtd(v_c), lg_fwd, lg_bwd, zero, zero)
    o_l, _, _ = bidir_retention(bhtd(q_l), bhtd(k_l) * k_scale, bhtd(v_l), lg_fwd, lg_bwd, s_f, s_b)

    def finish(o, g):
        t = o.shape[2]
        o = rms_norm(jnp.swapaxes(o, 1, 2)).reshape(b, t, ODD_MIX_DIM).astype(g.dtype)
        return (jax.nn.silu(g) * o) @ w_out

    out_l = finish(o_l, g_l)
    out_c = finish(o_c, g_c) if need_ctx else None
    return out_c, out_l


def swiglu(u, w_in, w_out):
    gate, up = jnp.split(u @ w_in, 2, axis=-1)
    return (jax.nn.silu(gate) * up) @ w_out


def setup_inputs(seed: int = 0) -> dict:
    key = jax.random.key(seed)
    ks = jax.random.split(key, 17)

    def dense(k, shape, fan_in):
        return jax.random.normal(k, shape, F32) * (fan_in ** -0.5)

    ret_base = jnp.log1p(-jnp.exp2(-5.0 - jnp.arange(RET_HEADS, dtype=F32)))
    return {
        'x': jax.random.normal(ks[0], (BATCH, SEQ, D_MODEL), F32),
        'c': jax.random.normal(ks[1], (BATCH, D_MODEL), F32),
        'ctx': jax.random.normal(ks[2], (BATCH, CTX_LEN, D_MODEL), F32),
        'c_ctx': jax.random.normal(ks[3], (D_MODEL,), F32),
        'w_mod': dense(ks[4], (DEPTH, D_MODEL, N_MOD * D_MODEL), D_MODEL),
        'b_mod': 0.01 * jax.random.normal(ks[5], (DEPTH, N_MOD * D_MODEL), F32),
        'w_in_even': dense(ks[6], (N_EVEN, D_MODEL, EVEN_IN_DIM), D_MODEL),
        'w_out_even': dense(ks[7], (N_EVEN, EVEN_MIX_DIM, D_MODEL), EVEN_MIX_DIM),
        'q_gain_even': 1.0 + 0.02 * jax.random.normal(ks[8], (N_EVEN, ATTN_HEAD_DIM), F32),
        'k_gain_even': 1.0 + 0.02 * jax.random.normal(ks[9], (N_EVEN, ATTN_HEAD_DIM), F32),
        'w_in_odd': dense(ks[10], (N_ODD, D_MODEL, ODD_IN_DIM), D_MODEL),
        'w_out_odd': dense(ks[11], (N_ODD, ODD_MIX_DIM, D_MODEL), ODD_MIX_DIM),
        'log_decay_fwd': ret_base[None] * (1.0 + 0.05 * jax.random.normal(ks[12], (N_ODD, RET_HEADS), F32)),
        'log_decay_bwd': ret_base[None] * (1.0 + 0.05 * jax.random.normal(ks[13], (N_ODD, RET_HEADS), F32)),
        'w_ffn_in': dense(ks[14], (DEPTH, D_MODEL, 2 * FFN_DIM), D_MODEL),
        'w_ffn_out': dense(ks[15], (DEPTH, FFN_DIM, D_MODEL), FFN_DIM),
    }


def reference(x, c, ctx, c_ctx, w_mod, b_mod, w_in_even, w_out_even, q_gain_even, k_gain_even,
              w_in_odd, w_out_odd, log_decay_fwd, log_decay_bwd, w_ffn_in, w_ffn_out):
    h_lat, h_ctx = x, ctx
    cond_lat = jax.nn.silu(c)
    cond_ctx = jax.nn.silu(c_ctx)[None]
    for i in range(DEPTH):
        need_ctx = i < DEPTH - 1
        mod_l = (cond_lat @ w_mod[i] + b_mod[i])[:, None, :]
        mod_c = (cond_ctx @ w_mod[i] + b_mod[i])[:, None, :]
        sh1, sc1, g1, sh2, sc2, g2 = jnp.split(mod_l, N_MOD, axis=-1)
        csh1, csc1, cg1, csh2, csc2, cg2 = jnp.split(mod_c, N_MOD, axis=-1)
        u_l = modulate(h_lat, sh1, sc1)
        u_c = modulate(h_ctx, csh1, csc1)
        j = i // 2
        if i % 2 == 0:
            o_c, o_l = fourier_gqa_mixer(u_c, u_l, w_in_even[j], w_out_even[j],
                                         q_gain_even[j], k_gain_even[j], need_ctx)
        else:
            o_c, o_l = retention_mixer(u_c, u_l, w_in_odd[j], w_out_odd[j],
                                       log_decay_fwd[j], log_decay_bwd[j], need_ctx)
        h_lat = h_lat + g1 * o_l
        h_lat = h_lat + g2 * swiglu(modulate(h_lat, sh2, sc2), w_ffn_in[i], w_ffn_out[i])
        if need_ctx:
            h_ctx = h_ctx + cg1 * o_c
            h_ctx = h_ctx + cg2 * swiglu(modulate(h_ctx, csh2, csc2), w_ffn_in[i], w_ffn_out[i])
    return h_lat
```

```python
import numpy as np
from contextlib import ExitStack
import concourse.bass as bass
import concourse.mybir as mybir
from concourse.bass_utils import run_bass_kernel_spmd

F32 = mybir.dt.float32
BF16 = mybir.dt.bfloat16
AF = mybir.ActivationFunctionType
ALU = mybir.AluOpType
AX = mybir.AxisListType

NCORES = 8
D = 2048
KC = 16
NL = 2048
NX = 256
NT = NL + NX
NTILE = NT // 128
FF = 5632
FC = FF // 128
DEPTH = 4
EPS = 1e-6
NDS = 12

SBLOCKS = [
    (0, [(0, 512, 0), (512, 256, 0)]),
    (768, [(0, 512, 0), (512, 256, 0)]),
    (1536, [(0, 512, 0), (512, 256, 1)]),
]


class Buf:
    __slots__ = ("w", "r", "name")

    def __init__(self, name=""):
        self.w = {}
        self.r = {}
        self.name = name


class Eng:
    def __init__(self, e, sem, inorder=False):
        self.e = e
        self.sem = sem
        self.cnt = 0
        self.seen = {}
        self.inorder = inorder


class Prog:
    def __init__(self, nc, st):
        self.nc = nc
        mk = lambda n: st.enter_context(nc.semaphore(n))
        self.E = {
            "pe": Eng(nc.tensor, mk("s_pe"), True),
            "act": Eng(nc.scalar, mk("s_act")),
            "dve": Eng(nc.vector, mk("s_dve")),
            "pool": Eng(nc.gpsimd, mk("s_pool")),
            "sp": Eng(nc.sync, mk("s_sp")),
        }
        self.dsem = {q: [[mk("d_%s%d" % (q, i)), 0] for i in range(NDS)] for q in ("sp", "pool", "act")}
        self.dptr = {"sp": 0, "pool": 0, "act": 0}
        self.nins = 0

    def _need(self, E, reads, writes):
        need = {}

        def mg(d):
            for k, t in d.items():
                if k not in need or need[k][1] < t[1]:
                    need[k] = t

        for b in reads:
            mg(b.w)
        for b in writes:
            mg(b.w)
            mg(b.r)
        for k, (sem, v) in need.items():
            if E.inorder and sem is E.sem:
                continue
            if E.seen.get(k, 0) >= v:
                continue
            E.e.wait_ge(sem, v)
            E.seen[k] = v

    def _mark(self, t, reads, writes):
        k = id(t[0])
        for b in reads:
            if k not in b.r or b.r[k][1] < t[1]:
                b.r[k] = t
        for b in writes:
            b.w = {k: t}
            b.r = {}

    def op(self, en, fn, reads=(), writes=()):
        E = self.E[en]
        self._need(E, reads, writes)
        ins = fn(E.e)
        E.cnt += 1
        ins.then_inc(E.sem, 1)
        self._mark((E.sem, E.cnt), reads, writes)
        self.nins += 1

    def mm(self, out_ap, outbuf, items, reads, start=True, stop=True):
        E = self.E["pe"]
        self._need(E, reads, [outbuf])
        n = len(items)
        ins = None
        for i, (l, r) in enumerate(items):
            ins = E.e.matmul(out_ap, lhsT=l, rhs=r, start=(start and i == 0), stop=(stop and i == n - 1))
        E.cnt += 1
        ins.then_inc(E.sem, 1)
        self._mark((E.sem, E.cnt), reads, [outbuf])
        self.nins += n

    def tr(self, outbuf, items, reads, ident):
        E = self.E["pe"]
        self._need(E, reads, [outbuf])
        ins = None
        for o, i in items:
            ins = E.e.transpose(out=o, in_=i, identity=ident)
        E.cnt += 1
        ins.then_inc(E.sem, 1)
        self._mark((E.sem, E.cnt), reads, [outbuf])
        self.nins += len(items)

    def dma(self, q, out, in_, reads=(), writes=(), **kw):
        E = self.E[q]
        self._need(E, reads, writes)
        lst = self.dsem[q]
        i = self.dptr[q]
        self.dptr[q] = (i + 1) % len(lst)
        sem, cnt = lst[i]
        k = id(sem)
        if cnt > 0 and E.seen.get(k, 0) < cnt:
            E.e.wait_ge(sem, cnt)
            E.seen[k] = cnt
        ins = E.e.dma_start(out=out, in_=in_, **kw)
        ins.then_inc(sem, 16)
        lst[i][1] = cnt + 16
        self._mark((sem, cnt + 16), reads, writes)
        self.nins += 1

    def barrier(self):
        for E in self.E.values():
            for E2 in self.E.values():
                if E2 is E or E2.cnt == 0:
                    continue
                k = id(E2.sem)
                if E.seen.get(k, 0) < E2.cnt:
                    E.e.wait_ge(E2.sem, E2.cnt)
                    E.seen[k] = E2.cnt
            for q in self.dsem:
                for sem, cnt in self.dsem[q]:
                    k = id(sem)
                    if cnt > 0 and E.seen.get(k, 0) < cnt:
                        E.e.wait_ge(sem, cnt)
                        E.seen[k] = cnt


class Rot:
    def __init__(self, items):
        self.items = items
        self.i = 0

    def next(self):
        x = self.items[self.i]
        self.i = (self.i + 1) % len(self.items)
        return x


def build(stop_after=None, dump_h=False):
    nc = bass.Bass("TRN2", target_bir_lowering=False)
    dt_in = lambda name, shape: nc.dram_tensor(name, shape, F32, kind="ExternalInput").ap()
    x_d = dt_in("x", [NL, D])
    ctx_d = dt_in("ctx", [NX, D])
    cT_d = dt_in("cT", [128, KC])
    cxT_d = dt_in("cxT", [128, KC])
    wmod_d = dt_in("w_mod", [DEPTH, D, 6 * D])
    bmodT_d = dt_in("bmodT", [128, DEPTH, 96])
    wine_d = dt_in("w_in_even", [2, D, 3072])
    woute_d = dt_in("w_out_even", [2, D, D])
    qg_d = dt_in("qg_rep", [2, 128, 512])
    kg_d = dt_in("kg_rep", [2, 128, 512])
    wino_d = dt_in("w_in_odd", [2, D, 12288])
    wouto_d = dt_in("w_out_odd", [2, 4096, D])
    lgf_d = dt_in("lgf_rep", [2, 128, 8])
    lgb_d = dt_in("lgb_rep", [2, 128, 8])
    wfi_d = dt_in("w_ffn_in", [DEPTH, D, 2 * FF])
    wfo_d = dt_in("w_ffn_out", [DEPTH, FF, D])
    ropeE_d = dt_in("ropeE", [NL, 256])
    ropeO_d = dt_in("ropeO", [NL, 512])
    dftc_d = dt_in("dftc", [NL, NL])
    dfts_d = dt_in("dfts", [NL, NL])
    dftc256_d = dt_in("dftc256", [NX, NX])
    dfts256_d = dt_in("dfts256", [NX, NX])
    dftch_d = dt_in("dftch", [128, 256])
    dftchx_d = dt_in("dftchx", [128, 256])
    rett_d = dt_in("rett", [128, 6, 128])
    retc_d = dt_in("retc", [128, 2])
    y_d = nc.dram_tensor("y", [NL, D], F32, kind="ExternalOutput").ap()
    hA = nc.dram_tensor("hA", [D, NT], F32).ap()
    hB = nc.dram_tensor("hB", [D, NT], F32).ap()
    QKT_d = nc.dram_tensor("QKT", [4096, NT], BF16).ap()
    KTOK_d = nc.dram_tensor("KTOK", [NT, 2048], BF16).ap()
    V_d = nc.dram_tensor("VTOK", [NT, 4096], BF16).ap()
    SG_d = nc.dram_tensor("SGTOK", [NT, 4096], BF16).ap()
    F_d = nc.dram_tensor("FTOK", [NT, 512], BF16).ap()
    YT_d = nc.dram_tensor("YT", [4096, NT], BF16).ap()
    if dump_h:
        dbg_d = nc.dram_tensor("dbg", [D, NT], F32, kind="ExternalOutput").ap()

    st = ExitStack()
    with st:
        P = Prog(nc, st)
        _uid = [0]

        def sb(name, shape, dt, s=st):
            _uid[0] += 1
            return s.enter_context(nc.sbuf_tensor("sb%d_%s" % (_uid[0], name), shape, dt))
        identF = sb("identF", [128, 128], F32)
        identB = sb("identB", [128, 128], BF16)
        onesB = sb("onesB", [128, 128], BF16)
        onesF = sb("onesF", [128, 128], F32)
        modsb = sb("modsb", [128, DEPTH, 96, 2], F32)
        epsT = sb("epsT", [128, 1], F32)
        b_const = Buf("const")
        b_modl = [Buf("mod%d" % i) for i in range(DEPTH)]
        condT = sb("condT", [128, KC, 2], BF16)
        bT = sb("bT", [128, DEPTH, 96], F32)
        cb = Buf("cond")
        psF = [st.enter_context(nc.psum_tensor("psF%d" % i, [128, 512], F32)) for i in range(6)]
        psFb = [Buf("psF%d" % i) for i in range(6)]
        psT = [st.enter_context(nc.psum_tensor("psT%d" % i, [128, 1024], BF16)) for i in range(2)]
        psTb = [Buf("psT%d" % i) for i in range(2)]

        P.op("pool", lambda e: e.memset(identF[:], 0.0), writes=[b_const])
        P.op("pool", lambda e: e.affine_select(out=identF[:], in_=identF[:], pattern=[[-1, 128]], base=0,
                                                channel_multiplier=1, compare_op=ALU.not_equal, fill=1.0),
             writes=[b_const])
        P.op("dve", lambda e: e.tensor_copy(out=identB[:], in_=identF[:]), reads=[b_const], writes=[b_const])
        P.op("dve", lambda e: e.memset(onesB[:], 1.0), writes=[b_const])
        P.op("dve", lambda e: e.memset(onesF[:], 1.0), writes=[b_const])
        P.op("dve", lambda e: e.memset(epsT[:], EPS), writes=[b_const])

        def done(tag):
            return stop_after is not None and stop_after == tag

        def phase_init(inner=None):
            with ExitStack() as ph:
                xin = [sb("xin%d" % i, [128, D], F32, ph) for i in range(2)]
                xinb = [Buf() for _ in range(2)]
                stg = [sb("xstg%d" % i, [128, KC, 128], F32, ph) for i in range(2)]
                stgb = [Buf() for _ in range(2)]
                prot = Rot([0, 1])

                def tiles():
                  for tt in range(NTILE):
                    yield from tile(tt)

                def tile(tt):
                    src = x_d[tt * 128:(tt + 1) * 128, :] if tt < 16 else ctx_d[(tt - 16) * 128:(tt - 15) * 128, :]
                    xi, xb = xin[tt % 2], xinb[tt % 2]
                    sg_, sgb = stg[tt % 2], stgb[tt % 2]
                    P.dma("sp", xi[:], src, writes=[xb])
                    for cg in range(4):
                        pi = prot.next()
                        P.tr(psFb[pi], [(psF[pi][:, i * 128:(i + 1) * 128], xi[:, (cg * 4 + i) * 128:(cg * 4 + i + 1) * 128])
                                        for i in range(4)], [xb, b_const], identF[:])
                        en = "act" if cg % 2 == 0 else "dve"
                        dst = sg_[:, cg * 4:(cg + 1) * 4, :]
                        srcp = psF[pi][:].rearrange("p (c t) -> p c t", c=4)
                        if en == "act":
                            P.op("act", lambda e, d=dst, s=srcp: e.copy(out=d, in_=s), reads=[psFb[pi]], writes=[sgb])
                        else:
                            P.op("dve", lambda e, d=dst, s=srcp: e.tensor_copy(out=d, in_=s), reads=[psFb[pi]], writes=[sgb])
                    P.dma("sp", hA[:, tt * 128:(tt + 1) * 128].rearrange("(c p) t -> p c t", p=128), sg_[:], reads=[sgb])
                    yield

                gen = tiles()
                if inner is not None:
                    inner(gen)
                for _ in gen:
                    pass
                P.barrier()

        def load_w(wt, wb, w_ap, col0, ncols, dstcol=0):
            K = w_ap.shape[0]
            P.dma("pool", wt[:, 0:K // 128, dstcol:dstcol + ncols],
                  w_ap[:, col0:col0 + ncols].rearrange("(c p) n -> p c n", p=128), writes=[wb])

        def mod_slab(l, s_, wt, wb, pbank, ncols=512, do_load=True, do_compute=True):
            wv = wt[:, 0:KC * ncols].rearrange("p (c n) -> p c n", n=ncols)
            if do_load:
                P.dma("pool", wv, wmod_d[l][:, s_ * ncols:(s_ + 1) * ncols].rearrange("(c p) n -> p c n", p=128), writes=[wb])
            if not do_compute:
                return
            for j in range(ncols // 128):
                n = s_ * (ncols // 128) + j
                pt_, pb_ = pbank()
                P.mm(pt_, pb_, [(wv[:, kc, j * 128:(j + 1) * 128], condT[:, kc, :]) for kc in range(KC)], [wb, cb])
                P.op("dve", lambda e: e.tensor_scalar(out=modsb[:, l, n, :], in0=pt_, scalar1=bT[:, l, n:n + 1], scalar2=None,
                                                       op0=ALU.add), reads=[pb_, cb], writes=[b_modl[l]])
            if (s_ + 1) * ncols == 6 * D:
                for slot in (1, 4):
                    P.op("dve", lambda e: e.tensor_scalar(
                        out=modsb[:, l, slot * 16:(slot + 1) * 16, :], in0=modsb[:, l, slot * 16:(slot + 1) * 16, :],
                        scalar1=1.0, scalar2=None, op0=ALU.add), reads=[b_modl[l]], writes=[b_modl[l]])

        def phase_mod(gen=None):
            with ExitStack() as ph:
                cT = sb("cT", [128, 2, KC], F32, ph)
                wts = [sb("wm%d" % i, [128, KC * 512], BF16, ph) for i in range(2)]
                wbs = [Buf() for _ in range(2)]
                P.dma("sp", cT[:, 0, :], cT_d, writes=[cb])
                P.dma("sp", cT[:, 1, :], cxT_d, writes=[cb])
                P.dma("sp", bT[:], bmodT_d, writes=[cb])
                P.op("act", lambda e: e.activation(out=condT[:].rearrange("p k j -> p j k"), in_=cT[:], func=AF.Silu), reads=[cb],
                     writes=[cb])
                prot = Rot([2, 3])

                def pbank():
                    pi = prot.next()
                    return psF[pi][:, 0:2], psFb[pi]

                for s_ in range(24):
                    mod_slab(0, s_, wts[s_ % 2], wbs[s_ % 2], pbank)
                    if gen is not None:
                        next(gen, None)
                if gen is not None:
                    for _ in gen:
                        pass
                P.barrier()

        bg_list = [(l, s_) for l in range(1, DEPTH) for s_ in range(48)]
        bg_pos = [0]
        bg_rot = Rot([0, 1])
        bg_only1 = [False]

        def bg_bank():
            ti = 1 if bg_only1[0] else bg_rot.next()
            return psT[ti][:].bitcast(F32)[:, 0:2], psTb[ti]

        bg_loaded = []

        def bg_load(wt, wb, maxlayer):
            if bg_pos[0] >= len(bg_list) or bg_list[bg_pos[0]][0] > maxlayer:
                return False
            l, s_ = bg_list[bg_pos[0]]
            bg_pos[0] += 1
            mod_slab(l, s_, wt, wb, bg_bank, 256, True, False)
            bg_loaded.append((l, s_, wt, wb))
            return True

        def bg_compute():
            if not bg_loaded:
                return False
            l, s_, wt, wb = bg_loaded.pop(0)
            mod_slab(l, s_, wt, wb, bg_bank, 256, False, True)
            return True

        def bg_step(wt, wb, upto_layer=None):
            if bg_pos[0] >= len(bg_list):
                return False
            l, s_ = bg_list[bg_pos[0]]
            bg_pos[0] += 1
            mod_slab(l, s_, wt, wb, bg_bank, 256)
            return True

        class ModRes:
            def __init__(self, ph, tag, nh=2):
                self.hin = [sb("hin%s%d" % (tag, i), [128, KC, 256], F32, ph) for i in range(nh)]
                self.hinb = [Buf() for _ in range(nh)]
                self.sq = sb("sq" + tag, [128, KC, 256], BF16, ph)
                self.sqb = Buf()
                self.sd = sb("sd" + tag, [128, 256], F32, ph)
                self.sdb = Buf()
                self.i = 0

        def modulate(mr, hsrc, tok0, l, slot_shift, slot_scale, j, uT, ub, off, pi):
            n = 256
            hi, hb = mr.hin[mr.i % len(mr.hin)], mr.hinb[mr.i % len(mr.hin)]
            mr.i += 1
            P.dma("sp", hi[:], hsrc[:, tok0:tok0 + n].rearrange("(c p) t -> p c t", p=128), writes=[hb])
            P.op("act", lambda e: e.activation(out=mr.sq[:], in_=hi[:], func=AF.Square), reads=[hb], writes=[mr.sqb])
            yield
            P.mm(psF[pi][:, 0:n], psFb[pi], [(onesB[:], mr.sq[:, c, :]) for c in range(KC)], [mr.sqb, b_const])
            P.op("act", lambda e: e.activation(out=mr.sd[:], in_=psF[pi][:, 0:n], func=AF.Sqrt, scale=1.0 / D, bias=epsT[:]),
                 reads=[psFb[pi], b_const], writes=[mr.sdb])
            P.op("dve", lambda e: e.reciprocal(out=mr.sd[:], in_=mr.sd[:]), reads=[mr.sdb], writes=[mr.sdb])
            P.op("dve", lambda e: e.tensor_tensor(out=hi[:], in0=hi[:], in1=mr.sd[:].unsqueeze(1).to_broadcast([128, KC, n]),
                                                   op=ALU.mult), reads=[hb, mr.sdb], writes=[hb])
            for c in range(KC):
                P.op("act", lambda e, c=c: e.activation(
                    out=uT[:, c, off:off + n], in_=hi[:, c, :], func=AF.Identity,
                    scale=modsb[:, l, slot_scale * 16 + c, j:j + 1], bias=modsb[:, l, slot_shift * 16 + c, j:j + 1]),
                    reads=[hb, b_modl[l]], writes=[ub])

        def modulate_gen(mr, hsrc, sbi, l, slot_shift, slot_scale, uT, ub):
            tok0, _ = SBLOCKS[sbi]
            for u in range(3):
                g0 = tok0 + u * 256
                for _ in modulate(mr, hsrc, g0, l, slot_shift, slot_scale, 1 if g0 >= NL else 0, uT, ub, u * 256, 4 + (u % 2)):
                    yield
                yield

        def modulate_block(mr, hsrc, sbi, l, slot_shift, slot_scale, uT, ub):
            for _ in modulate_gen(mr, hsrc, sbi, l, slot_shift, slot_scale, uT, ub):
                pass

        def linear_residual(ph, w_ap, src, srcb, tok0, pieces, hsrc, hdst, l, gslot, need_ctx, wts, wbs, wctr, hook=None,
                            store_q="sp"):
            K = w_ap.shape[0]
            kc = K // 128
            ncols = 256
            ht = ph["ht"]
            for s in range(D // ncols):
                wt, wb = wts[wctr[0] % 2], wbs[wctr[0] % 2]
                wctr[0] += 1
                wv = wt[:, 0:kc * ncols].rearrange("p (c n) -> p c n", n=ncols)
                P.dma("pool", wv, w_ap[:, s * ncols:(s + 1) * ncols].rearrange("(c p) n -> p c n", p=128), writes=[wb])
                for jj in range(ncols // 128):
                    nch = s * (ncols // 128) + jj
                    for (o, n, isctx) in pieces:
                        if isctx and not need_ctx:
                            continue
                        pi = ph["prot"].next()
                        g0 = tok0 + o
                        hi_t, hi_b, ho_t, ho_b = ht.next()
                        P.dma("sp", hi_t[:, 0:n], hsrc[nch * 128:(nch + 1) * 128, g0:g0 + n], writes=[hi_b])
                        P.mm(psF[pi][:, 0:n], psFb[pi],
                             [(wv[:, c, jj * 128:(jj + 1) * 128], src[:, c, o:o + n]) for c in range(kc)],
                             [wb, srcb[o] if isinstance(srcb, dict) else srcb])
                        P.op("dve", lambda e, pi=pi, n=n, nch=nch, isctx=isctx, hi_t=hi_t, ho_t=ho_t: e.scalar_tensor_tensor(
                            out=ho_t[:, 0:n], in0=psF[pi][:, 0:n], scalar=modsb[:, l, gslot * 16 + nch, isctx:isctx + 1],
                            in1=hi_t[:, 0:n], op0=ALU.mult, op1=ALU.add), reads=[psFb[pi], hi_b, b_modl[l]], writes=[ho_b])
                        P.dma(store_q, hdst[nch * 128:(nch + 1) * 128, g0:g0 + n], ho_t[:, 0:n], reads=[ho_b])
                    if hook is not None:
                        hook()

        def mk_ht(ph_stack, tag, n=3):
            items = []
            for i in range(n):
                items.append((sb("hi%s%d" % (tag, i), [128, 512], F32, ph_stack), Buf(),
                              sb("ho%s%d" % (tag, i), [128, 512], F32, ph_stack), Buf()))
            return Rot(items)

        def phase_ffn(l, need_ctx):
            with ExitStack() as ph:
                actT = sb("actT", [128, FC, 768], BF16, ph)
                actb = Buf()
                uT = sb("uTf", [128, KC, 768], BF16, ph)
                ub = Buf()
                mr = ModRes(ph, "f", 1)
                wts = [sb("wf%d" % i, [128, FC * 256], BF16, ph) for i in range(2)]
                wbs = [Buf() for _ in range(2)]
                wctr = [0]
                sgs = Rot([(sb("sgf%d" % i, [128, 512], F32, ph), Buf()) for i in range(2)])
                phd = {"ht": mk_ht(ph, "f"), "prot": Rot(list(range(4)))}
                prot = Rot([(0, 1), (2, 3)])

                def nextw():
                    i = wctr[0] % 2
                    wctr[0] += 1
                    return wts[i], wbs[i]

                modulate_block(mr, hB, 0, l, 3, 4, uT, ub)
                for sbi, (tok0, pieces) in enumerate(SBLOCKS):
                    for s_ in range(FC // 2):
                        wt, wb = nextw()
                        wv = wt[:, 0:KC * 512].rearrange("p (c n) -> p c n", n=512)
                        P.dma("pool", wv[:, :, 0:256], wfi_d[l][:, s_ * 256:(s_ + 1) * 256].rearrange("(c p) n -> p c n", p=128),
                              writes=[wb])
                        P.dma("pool", wv[:, :, 256:512],
                              wfi_d[l][:, FF + s_ * 256:FF + (s_ + 1) * 256].rearrange("(c p) n -> p c n", p=128), writes=[wb])
                        for jj in range(2):
                            j = 2 * s_ + jj
                            for (o, n, isctx) in pieces:
                                if isctx and not need_ctx:
                                    continue
                                pg, pu = prot.next()
                                P.mm(psF[pg][:, 0:n], psFb[pg],
                                     [(wv[:, c, jj * 128:(jj + 1) * 128], uT[:, c, o:o + n]) for c in range(KC)], [wb, ub])
                                P.mm(psF[pu][:, 0:n], psFb[pu],
                                     [(wv[:, c, 256 + jj * 128:256 + (jj + 1) * 128], uT[:, c, o:o + n]) for c in range(KC)],
                                     [wb, ub])
                                sgt, sgb = sgs.next()
                                P.op("act", lambda e: e.activation(out=sgt[:, 0:n], in_=psF[pg][:, 0:n], func=AF.Silu),
                                     reads=[psFb[pg]], writes=[sgb])
                                P.op("dve", lambda e: e.tensor_tensor(out=actT[:, j, o:o + n], in0=sgt[:, 0:n], in1=psF[pu][:, 0:n],
                                                                       op=ALU.mult), reads=[psFb[pu], sgb], writes=[actb])
                    hook = None
                    if sbi + 1 < len(SBLOCKS):
                        gen = modulate_gen(mr, hB, sbi + 1, l, 3, 4, uT, ub)
                        hook = lambda gen=gen: next(gen, None)
                    linear_residual(phd, wfo_d[l], actT, actb, tok0, pieces, hB, hA, l, 5, need_ctx, wts, wbs, wctr, hook)
                    if hook is not None:
                        for _ in gen:
                            pass
                P.barrier()

        def phase_out(l, w_ap, need_ctx):
            K = w_ap.shape[0]
            kc = K // 128
            with ExitStack() as ph:
                yT = sb("yTo", [128, kc, NT], BF16, ph)
                pieces = [(0, 512, 0), (512, 512, 0), (1024, 512, 0), (1536, 512, 0), (2048, 256, 1)]
                ybs = {}
                for (o, n, isctx) in pieces:
                    if isctx and not need_ctx:
                        continue
                    ybs[o] = Buf()
                    for c0 in range(0, kc, 16):
                        P.dma("sp", yT[:, c0:c0 + 16, o:o + n],
                              YT_d[c0 * 128:(c0 + 16) * 128, o:o + n].rearrange("(c p) t -> p c t", p=128), writes=[ybs[o]])
                wts = [sb("wo%d" % i, [128, kc * 256], BF16, ph) for i in range(2)]
                wbs = [Buf() for _ in range(2)]
                phd = {"ht": mk_ht(ph, "o", 6 if kc == 16 else 3), "prot": Rot(list(range(4)))}
                linear_residual(phd, w_ap, yT, ybs, 0, pieces, hA, hB, l, 2, need_ctx, wts, wbs, [0], None, "act")
                if not need_ctx:
                    pass
                P.barrier()

        def rope_tm(src_ap, src_buf, tab, tabb, gt, nh, hd, t1, t1b, t2, t2b, out_ap, outb):
            q = hd // 4
            cosb = tab[:, gt, 0:hd].unsqueeze(1).to_broadcast([128, nh, hd])
            P.op("dve", lambda e: e.tensor_tensor(out=t1[:, 0:nh * hd].rearrange("p (h d) -> p h d", h=nh),
                                                   in0=src_ap.rearrange("p (h d) -> p h d", h=nh), in1=cosb, op=ALU.mult),
                 reads=[src_buf, tabb], writes=[t1b])
            s5 = src_ap.rearrange("p (h a j d) -> p h a j d", h=nh, a=2, j=2)
            o5 = t2[:, 0:nh * hd].rearrange("p (h a j d) -> p h a j d", h=nh, a=2, j=2)
            sn = tab[:, gt, hd:2 * hd].rearrange("p (a j d) -> p a j d", a=2, j=2)
            for jx in range(2):
                sinb = sn[:, :, jx, :].unsqueeze(1).to_broadcast([128, nh, 2, q])
                P.op("dve", lambda e, jx=jx, sinb=sinb: e.tensor_tensor(out=o5[:, :, :, jx, :], in0=s5[:, :, :, 1 - jx, :],
                                                                       in1=sinb, op=ALU.mult),
                     reads=[src_buf, tabb], writes=[t2b])
            P.op("dve", lambda e: e.tensor_tensor(out=out_ap, in0=t1[:, 0:nh * hd], in1=t2[:, 0:nh * hd], op=ALU.add),
                 reads=[t1b, t2b], writes=[outb])

        def phase_proj_even(l, j2, need_ctx):
            with ExitStack() as ph:
                tab = sb("ropeE", [128, 16, 256], F32, ph)
                tabb = Buf()
                P.dma("sp", tab[:], ropeE_d.rearrange("(t p) n -> p t n", p=128), writes=[tabb])
                gq = sb("gq", [128, 512], F32, ph)
                gk = sb("gk", [128, 512], F32, ph)
                gb = Buf()
                P.dma("sp", gq[:], qg_d[j2], writes=[gb])
                P.dma("sp", gk[:], kg_d[j2], writes=[gb])
                uTs = [(sb("uTe%d" % i, [128, KC, 768], BF16, ph), Buf()) for i in range(2)]
                mr = ModRes(ph, "e", 1)
                wts = [sb("we%d" % i, [128, KC, 512], BF16, ph) for i in range(2)]
                wbs = [Buf() for _ in range(2)]
                qkst = sb("qkst", [128, 16, 768], BF16, ph)
                qkstb = Buf()
                tokst = [sb("tokst%d" % i, [128, 6, 512], BF16, ph) for i in range(2)]
                tokstb = [Buf() for _ in range(2)]
                tmps = Rot([(sb("sqe%d" % i, [128, 512], F32, ph), Buf(), sb("ssqe%d" % i, [128, 4], F32, ph), Buf(),
                             sb("qne%d" % i, [128, 512], F32, ph), Buf(), sb("t1e%d" % i, [128, 512], F32, ph), Buf(),
                             sb("t2e%d" % i, [128, 512], F32, ph), Buf()) for i in range(3)])
                qtoks = Rot([(sb("qtoke%d" % i, [128, 512], BF16, ph), Buf()) for i in range(4)])
                prot = Rot(list(range(4)))
                trot = Rot([0, 1])
                wi = 0
                pend = []

                def flush(keep):
                    while len(pend) > keep:
                        pend.pop(0)()

                modulate_block(mr, hA, 0, l, 0, 1, uTs[0][0], uTs[0][1])
                for sbi, (tok0, pieces) in enumerate(SBLOCKS):
                    uT, ub = uTs[sbi % 2]
                    gen = None
                    if sbi + 1 < len(SBLOCKS):
                        gen = modulate_gen(mr, hA, sbi + 1, l, 0, 1, uTs[(sbi + 1) % 2][0], uTs[(sbi + 1) % 2][1])
                    for s in range(6):
                        wt, wb = wts[wi % 2], wbs[wi % 2]
                        wi += 1
                        load_w(wt, wb, wine_d[j2], s * 512, 512)
                        if s == 0 or s == 5:
                            flush(0)
                        tks, tkb = tokst[s % 2], tokstb[s % 2]
                        for tl in range(6):
                            gt = tok0 // 128 + tl
                            isctx = gt >= 16
                            pi = prot.next()
                            P.mm(psF[pi][:], psFb[pi], [(uT[:, c, tl * 128:(tl + 1) * 128], wt[:, c, :]) for c in range(KC)],
                                 [ub, wb])
                            flush(2)
                            if gen is not None and 1 <= s <= 4:
                                next(gen, None)
                            if s == 0 or s == 5:
                                P.op("act", lambda e, pi=pi, tl=tl, tks=tks: e.copy(out=tks[:, tl, :], in_=psF[pi][:]),
                                     reads=[psFb[pi]], writes=[tkb])
                                continue
                            gain = gk if s == 4 else gq
                            sq, sqb, ssq, ssqb, qn, qnb, t1, t1b, t2, t2b = tmps.next()
                            P.op("act", lambda e, pi=pi: e.activation(out=sq[:], in_=psF[pi][:], func=AF.Square),
                                 reads=[psFb[pi]], writes=[sqb])
                            P.op("dve", lambda e: e.tensor_reduce(out=ssq[:], in_=sq[:].rearrange("p (h d) -> p h d", h=4),
                                                                   axis=AX.X, op=ALU.add), reads=[sqb], writes=[ssqb])
                            P.op("act", lambda e: e.activation(out=ssq[:], in_=ssq[:], func=AF.Sqrt, scale=1.0 / 128, bias=epsT[:]),
                                 reads=[ssqb, b_const], writes=[ssqb])
                            P.op("dve", lambda e: e.reciprocal(out=ssq[:], in_=ssq[:]), reads=[ssqb], writes=[ssqb])
                            P.op("dve", lambda e, pi=pi: e.tensor_tensor(
                                out=qn[:].rearrange("p (h d) -> p h d", h=4), in0=psF[pi][:].rearrange("p (h d) -> p h d", h=4),
                                in1=ssq[:].unsqueeze(2).to_broadcast([128, 4, 128]), op=ALU.mult),
                                reads=[psFb[pi], ssqb], writes=[qnb])
                            qt, qtb = qtoks.next()
                            if isctx:
                                P.op("dve", lambda e, gain=gain, qt=qt: e.tensor_tensor(out=qt[:], in0=qn[:], in1=gain[:], op=ALU.mult),
                                     reads=[qnb, gb], writes=[qtb])
                            else:
                                P.op("dve", lambda e, gain=gain: e.tensor_tensor(out=qn[:], in0=qn[:], in1=gain[:], op=ALU.mult),
                                     reads=[qnb, gb], writes=[qnb])
                                rope_tm(qn[:], qnb, tab, tabb, gt, 4, 128, t1, t1b, t2, t2b, qt[:], qtb)
                            def do_tr(qt=qt, qtb=qtb, h0=(s - 1) * 4, tl=tl):
                                ti = trot.next()
                                P.tr(psTb[ti], [(psT[ti][:, h * 128:(h + 1) * 128], qt[:, h * 128:(h + 1) * 128]) for h in range(4)],
                                     [qtb, b_const], identB[:])
                                P.op("act", lambda e: e.copy(out=qkst[:, h0:h0 + 4, tl * 128:(tl + 1) * 128],
                                                             in_=psT[ti][:, 0:512].rearrange("p (h t) -> p h t", h=4)),
                                     reads=[psTb[ti]], writes=[qkstb])

                            pend.append(do_tr)
                        if s == 0:
                            P.dma("sp", F_d[tok0:tok0 + 768, :].rearrange("(t p) n -> p t n", p=128), tks[:], reads=[tkb])
                        if s == 5:
                            P.dma("sp", V_d[tok0:tok0 + 768, 0:512].rearrange("(t p) n -> p t n", p=128), tks[:], reads=[tkb])
                    pend.append(lambda tok0=tok0: P.dma("sp", QKT_d[0:2048, tok0:tok0 + 768].rearrange("(h p) t -> p h t", p=128),
                                                        qkst[:], reads=[qkstb]))
                    if gen is not None:
                        for _ in gen:
                            pass
                flush(0)
                P.barrier()

        def phase_att(l, need_ctx):
            with ExitStack() as ph:
                kT = sb("kTa", [128, 4, NT], BF16, ph)
                vt = sb("vta", [128, NTILE, 512], BF16, ph)
                kvb = Buf()
                P.dma("sp", kT[:], QKT_d[12 * 128:16 * 128, :].rearrange("(g p) t -> p g t", p=128), writes=[kvb])
                P.dma("sp", vt[:], V_d[:, 0:512].rearrange("(t p) n -> p t n", p=128), writes=[kvb])
                qTs = Rot([(sb("qTa%d" % i, [128, NT], BF16, ph), Buf()) for i in range(2)])
                Es = Rot([(sb("Ea%d" % i, [128, 512], BF16, ph), Buf()) for i in range(4)])
                rden = sb("rden", [128, 512], F32, ph)
                rdb = Buf()
                asts = Rot([(sb("ast%d" % i, [128, 512], BF16, ph), Buf()) for i in range(2)])
                srot = Rot([0, 1, 2])
                orot = Rot([((psF[3][:], psFb[3]), (psF[4][:], psFb[4])),
                            ((psF[5][:], psFb[5]), (psT[0][:].bitcast(F32), psTb[0]))])
                bg_only1[0] = True
                qblocks = [(0, 512, 0), (512, 512, 0), (1024, 512, 0), (1536, 512, 0)]
                if need_ctx:
                    qblocks.append((2048, 256, 1))
                scale = 128.0 ** -0.5
                bgw = Rot([(sb("bgwa%d" % i, [128, KC * 256], BF16, ph), Buf()) for i in range(2)])
                for _ in range(2):
                    wt2, wb2 = bgw.next()
                    bg_load(wt2, wb2, l + 2)
                qnext = qTs.next()
                P.dma("sp", qnext[0][:], QKT_d[0:128, :], writes=[qnext[1]])
                for head in range(12):
                    g = head // 3
                    qT, qb = qnext
                    if head + 1 < 12:
                        qnext = qTs.next()
                        P.dma("sp", qnext[0][:], QKT_d[(head + 1) * 128:(head + 2) * 128, :], writes=[qnext[1]])
                    for (q0, qn_, isctx) in qblocks:
                        kts = [16, 17] if isctx else list(range(NTILE))
                        (po_ap, po_b), (pd_ap, pd_b) = orot.next()

                        def s_mm(kt):
                            pi = srot.next()
                            P.mm(psF[pi][:, 0:qn_], psFb[pi], [(kT[:, g, kt * 128:(kt + 1) * 128], qT[:, q0:q0 + qn_])], [kvb, qb])
                            return pi

                        sq_ = [s_mm(kt) for kt in kts[:2]]
                        for ix, kt in enumerate(kts):
                            pi = sq_.pop(0)
                            Et, Eb = Es.next()
                            P.op("act", lambda e: e.activation(out=Et[:, 0:qn_], in_=psF[pi][:, 0:qn_], func=AF.Exp, scale=scale),
                                 reads=[psFb[pi]], writes=[Eb])
                            if ix + 2 < len(kts):
                                sq_.append(s_mm(kts[ix + 2]))
                            first, last = ix == 0, ix == len(kts) - 1
                            P.mm(po_ap[:, 0:qn_], po_b, [(vt[:, kt, g * 128:(g + 1) * 128], Et[:, 0:qn_])], [kvb, Eb],
                                 start=first, stop=last)
                            P.mm(pd_ap[:, 0:qn_], pd_b, [(onesB[:], Et[:, 0:qn_])], [b_const, Eb], start=first, stop=last)
                        P.op("dve", lambda e: e.reciprocal(out=rden[:, 0:qn_], in_=pd_ap[:, 0:qn_]), reads=[pd_b], writes=[rdb])
                        at, ab = asts.next()
                        P.op("dve", lambda e: e.tensor_tensor(out=at[:, 0:qn_], in0=po_ap[:, 0:qn_], in1=rden[:, 0:qn_], op=ALU.mult),
                             reads=[po_b, rdb], writes=[ab])
                        P.dma("sp", YT_d[(4 + head) * 128:(5 + head) * 128, q0:q0 + qn_], at[:, 0:qn_], reads=[ab])
                        nb = 0
                        while bg_compute():
                            nb += 1
                        for _ in range(nb):
                            wt2, wb2 = bgw.next()
                            bg_load(wt2, wb2, l + 2)
                while bg_compute():
                    pass
                bg_only1[0] = False
                P.barrier()

        def phase_fourier(l, need_ctx, inner=None):
            with ExitStack() as ph:
                ft = sb("ftk", [128, NTILE, 512], BF16, ph)
                fb = Buf()
                P.dma("sp", ft[:], F_d.rearrange("(t p) n -> p t n", p=128), writes=[fb])
                cch = sb("cch", [128, 256], BF16, ph)
                cchb = Buf()
                P.dma("pool", cch[:], dftch_d, writes=[cchb])
                cchx = sb("cchx", [128, 256], BF16, ph)
                P.dma("pool", cchx[:], dftchx_d, writes=[cchb])
                slabs = Rot([(sb("dc%d" % i, [128, 16, 512], BF16, ph), sb("ds%d" % i, [128, 16, 512], BF16, ph), Buf())
                             for i in range(2)])
                zs = Rot([(sb("zc%d" % i, [128, 512], BF16, ph), sb("zs%d" % i, [128, 512], BF16, ph), Buf(), Buf())
                          for i in range(2)])
                ysts = Rot([(sb("yst%d" % i, [128, 512], BF16, ph), Buf()) for i in range(2)])
                prot = Rot([(0, 1, 2), (3, 4, 5)])
                blocks = [(0, 512, 0), (512, 512, 0), (1024, 512, 0), (1536, 512, 0)]
                if need_ctx:
                    blocks.append((2048, 256, 1))
                for (t0, n, isctx) in blocks:
                    dc, ds, db = slabs.next()
                    if isctx:
                        nch, tbase = 2, 16
                        P.dma("pool", dc[:, 0:2, 0:256], dftc256_d.rearrange("(c p) n -> p c n", p=128), writes=[db])
                        P.dma("pool", ds[:, 0:2, 0:256], dfts256_d.rearrange("(c p) n -> p c n", p=128), writes=[db])
                    else:
                        nch, tbase = 16, 0
                        P.dma("pool", dc[:], dftc_d[:, t0:t0 + 512].rearrange("(c p) n -> p c n", p=128), writes=[db])
                        P.dma("pool", ds[:], dfts_d[:, t0:t0 + 512].rearrange("(c p) n -> p c n", p=128), writes=[db])
                    for g in range(4):
                        pc, ps_, py = prot.next()
                        zc, zs_, zcb, zsb = zs.next()
                        P.mm(psF[pc][:, 0:n], psFb[pc], [(ft[:, tbase + c, g * 128:(g + 1) * 128], dc[:, c, 0:n]) for c in range(nch)],
                             [fb, db])
                        P.mm(psF[ps_][:, 0:n], psFb[ps_], [(ft[:, tbase + c, g * 128:(g + 1) * 128], ds[:, c, 0:n]) for c in range(nch)],
                             [fb, db])
                        P.op("act", lambda e, pc=pc, zc=zc: e.copy(out=zc[:, 0:n], in_=psF[pc][:, 0:n]), reads=[psFb[pc]], writes=[zcb])
                        P.op("dve", lambda e, ps_=ps_, zs_=zs_: e.tensor_copy(out=zs_[:, 0:n], in_=psF[ps_][:, 0:n]), reads=[psFb[ps_]],
                             writes=[zsb])
                        P.mm(psF[py][:, 0:n], psFb[py], [((cchx if isctx else cch)[:, 0:128], zc[:, 0:n]), ((cchx if isctx else cch)[:, 128:256], zs_[:, 0:n])],
                             [cchb, zcb, zsb])
                        yt_, ytb = ysts.next()
                        P.op("act", lambda e, py=py, yt_=yt_: e.copy(out=yt_[:, 0:n], in_=psF[py][:, 0:n]), reads=[psFb[py]],
                             writes=[ytb])
                        P.dma("sp", YT_d[g * 128:(g + 1) * 128, t0:t0 + n], yt_[:, 0:n], reads=[ytb])
                if inner is not None:
                    inner()
                else:
                    P.barrier()

        def phase_proj_odd(l, j2, need_ctx):
            with ExitStack() as ph:
                tab = sb("ropeO", [128, 16, 512], F32, ph)
                tabb = Buf()
                P.dma("sp", tab[:], ropeO_d.rearrange("(t p) n -> p t n", p=128), writes=[tabb])
                uTs = [(sb("uTo%d" % i, [128, KC, 768], BF16, ph), Buf()) for i in range(2)]
                mr = ModRes(ph, "o", 2)
                wts = [sb("wod%d" % i, [128, KC, 512], BF16, ph) for i in range(2)]
                wbs = [Buf() for _ in range(2)]
                qkst = [sb("qksto%d" % i, [128, 4, 768], BF16, ph) for i in range(2)]
                qkstb = [Buf() for _ in range(2)]
                tokst = [sb("toksto%d" % i, [128, 6, 512], BF16, ph) for i in range(2)]
                tokstb = [Buf() for _ in range(2)]
                tmps = Rot([(sb("t1o%d" % i, [128, 512], F32, ph), Buf(), sb("t2o%d" % i, [128, 512], F32, ph), Buf())
                            for i in range(3)])
                prot = Rot(list(range(4)))
                trot = Rot([0, 1])
                wi = 0
                pend = []

                def flush(keep):
                    while len(pend) > keep:
                        pend.pop(0)()

                modulate_block(mr, hA, 0, l, 0, 1, uTs[0][0], uTs[0][1])
                for sbi, (tok0, pieces) in enumerate(SBLOCKS):
                    uT, ub = uTs[sbi % 2]
                    gen = None
                    if sbi + 1 < len(SBLOCKS):
                        gen = modulate_gen(mr, hA, sbi + 1, l, 0, 1, uTs[(sbi + 1) % 2][0], uTs[(sbi + 1) % 2][1])
                    for s in range(24):
                        wt, wb = wts[wi % 2], wbs[wi % 2]
                        wi += 1
                        load_w(wt, wb, wino_d[j2], s * 512, 512)
                        if s >= 8 or s == 0:
                            flush(0)
                        tks, tkb = tokst[s % 2], tokstb[s % 2]
                        qks, qkb = qkst[s % 2], qkstb[s % 2]
                        for tl in range(6):
                            gt = tok0 // 128 + tl
                            isctx = gt >= 16
                            pi = prot.next()
                            P.mm(psF[pi][:], psFb[pi], [(uT[:, c, tl * 128:(tl + 1) * 128], wt[:, c, :]) for c in range(KC)],
                                 [ub, wb])
                            flush(2)
                            if gen is not None and s < 8 and tl % 2 == 0:
                                next(gen, None)
                            if s >= 16:
                                P.op("act", lambda e, pi=pi, tl=tl, tks=tks: e.activation(out=tks[:, tl, :], in_=psF[pi][:], func=AF.Silu),
                                     reads=[psFb[pi]], writes=[tkb])
                                continue
                            if s >= 8:
                                P.op("act", lambda e, pi=pi, tl=tl, tks=tks: e.copy(out=tks[:, tl, :], in_=psF[pi][:]),
                                     reads=[psFb[pi]], writes=[tkb])
                                continue
                            if isctx:
                                P.op("act", lambda e, pi=pi, tl=tl, tks=tks: e.copy(out=tks[:, tl, :], in_=psF[pi][:]),
                                     reads=[psFb[pi]], writes=[tkb])
                            else:
                                t1, t1b, t2, t2b = tmps.next()
                                rope_tm(psF[pi][:], psFb[pi], tab, tabb, gt, 2, 256, t1, t1b, t2, t2b, tks[:, tl, :], tkb)
                            def do_tr(tks=tks, tkb=tkb, qks=qks, qkb=qkb, tl=tl):
                                ti = trot.next()
                                P.tr(psTb[ti], [(psT[ti][:, h * 128:(h + 1) * 128], tks[:, tl, h * 128:(h + 1) * 128])
                                                for h in range(4)], [tkb, b_const], identB[:])
                                P.op("act", lambda e: e.copy(out=qks[:, :, tl * 128:(tl + 1) * 128],
                                                             in_=psT[ti][:, 0:512].rearrange("p (h t) -> p h t", h=4)),
                                     reads=[psTb[ti]], writes=[qkb])

                            pend.append(do_tr)
                        if s < 8:
                            pend.append(lambda s=s, tok0=tok0, qks=qks, qkb=qkb: P.dma(
                                "sp", QKT_d[s * 512:(s + 1) * 512, tok0:tok0 + 768].rearrange("(h p) t -> p h t", p=128), qks[:],
                                reads=[qkb]))
                        if 4 <= s < 8:
                            pend.append(lambda s=s, tok0=tok0, tks=tks, tkb=tkb: P.dma(
                                "sp", KTOK_d[tok0:tok0 + 768, (s - 4) * 512:(s - 3) * 512].rearrange("(t p) n -> p t n", p=128),
                                tks[:], reads=[tkb]))
                        if 8 <= s < 16:
                            P.dma("sp", V_d[tok0:tok0 + 768, (s - 8) * 512:(s - 7) * 512].rearrange("(t p) n -> p t n", p=128),
                                  tks[:], reads=[tkb])
                        if s >= 16:
                            P.dma("sp", SG_d[tok0:tok0 + 768, (s - 16) * 512:(s - 15) * 512].rearrange("(t p) n -> p t n", p=128),
                                  tks[:], reads=[tkb])
                    if gen is not None:
                        for _ in gen:
                            pass
                flush(0)
                P.barrier()

        def phase_ret(l, j2, need_ctx):
            with ExitStack() as ph:
                rett = sb("rett", [128, 6, 128], F32, ph)
                retc = sb("retc", [128, 2], F32, ph)
                lg = sb("lg", [128, 2, 8], F32, ph)
                cb = Buf()
                P.dma("sp", rett[:], rett_d, writes=[cb])
                P.dma("sp", retc[:], retc_d, writes=[cb])
                P.dma("sp", lg[:, 0, :], lgf_d[j2], writes=[cb])
                P.dma("sp", lg[:, 1, :], lgb_d[j2], writes=[cb])
                Mall = sb("Mall", [128, 8, 128], F32, ph)
                qdec = sb("qdec", [128, 2, 8, 128], F32, ph)
                kdec = sb("kdec", [128, 2, 8], F32, ph)
                cdec = sb("cdec", [128, 2, 8], F32, ph)
                e1 = sb("e1", [128, 128], F32, ph)
                e1b = Buf()
                tb = Buf()
                for h in range(8):
                    for d_ in range(2):
                        P.op("act", lambda e, h=h, d_=d_: e.activation(out=e1[:], in_=rett[:, 2 * d_, :], func=AF.Exp,
                                                                       scale=lg[:, d_, h:h + 1]), reads=[cb], writes=[e1b])
                        if d_ == 0:
                            P.op("dve", lambda e, h=h: e.tensor_tensor(out=Mall[:, h, :], in0=e1[:], in1=rett[:, 1, :], op=ALU.mult),
                                 reads=[e1b, cb], writes=[tb])
                        else:
                            P.op("dve", lambda e: e.tensor_tensor(out=e1[:], in0=e1[:], in1=rett[:, 3, :], op=ALU.mult),
                                 reads=[e1b, cb], writes=[e1b])
                            P.op("dve", lambda e, h=h: e.tensor_tensor(out=Mall[:, h, :], in0=Mall[:, h, :], in1=e1[:], op=ALU.add),
                                 reads=[e1b, tb], writes=[tb])
                        P.op("act", lambda e, h=h, d_=d_: e.activation(out=qdec[:, d_, h, :], in_=rett[:, 4 + d_, :], func=AF.Exp,
                                                                       scale=lg[:, d_, h:h + 1]), reads=[cb], writes=[tb])
                        P.op("act", lambda e, h=h, d_=d_: e.activation(out=kdec[:, d_, h:h + 1], in_=retc[:, d_:d_ + 1], func=AF.Exp,
                                                                       scale=lg[:, d_, h:h + 1]), reads=[cb], writes=[tb])
                        P.op("act", lambda e, h=h, d_=d_: e.activation(out=cdec[:, d_, h:h + 1], in_=lg[:, d_, h:h + 1], func=AF.Exp,
                                                                       scale=128.0), reads=[cb], writes=[tb])
                P.op("dve", lambda e: e.tensor_scalar(out=kdec[:], in0=kdec[:], scalar1=1.0 / 16, scalar2=None, op0=ALU.mult),
                     reads=[tb], writes=[tb])
                qT = sb("qTr", [128, 2, NT], BF16, ph)
                kT = sb("kTr", [128, 2, NT], BF16, ph)
                ktk = sb("ktkr", [128, NTILE, 256], BF16, ph)
                rawb = Buf()
                vtks = Rot([(sb("vtkr%d" % i, [128, NTILE, 512], BF16, ph), Buf()) for i in range(2)])
                sgt = sb("sgr", [128, NTILE, 512], BF16, ph)
                sgb = Buf()
                part = sb("partr", [128, NTILE, 512], F32, ph)
                partb = [Buf() for _ in range(NTILE)]
                S = [sb("Sst%d" % i, [128, 2, 512], F32, ph) for i in range(2)]
                Sb_ = [[Buf(), Buf()], [Buf(), Buf()]]
                Sbf = [Rot([(sb("Sbf%d_%d" % (d_, i), [128, 2, 512], BF16, ph), Buf()) for i in range(3)]) for d_ in range(2)]
                pTs = Rot([(sb("pTr%d" % i, [128, 128], BF16, ph), Buf()) for i in range(2)])
                ksr = [Rot([(sb("ksr%d_%d" % (d_, i), [128, 256], BF16, ph), Buf()) for i in range(3)]) for d_ in range(2)]
                qsr = [Rot([(sb("qsr%d_%d" % (d_, i), [128, 2, 128], BF16, ph), Buf()) for i in range(3)]) for d_ in range(2)]
                ots = Rot([(sb("otr%d" % i, [128, 512], F32, ph), Buf()) for i in range(7)])
                junks = Rot([(sb("junkr%d" % i, [128, 512], F32, ph), Buf()) for i in range(4)])
                ssqs = Rot([(sb("ssqr%d" % i, [128, 1], F32, ph), Buf()) for i in range(6)])
                yts = Rot([(sb("ytr%d" % i, [128, 512], BF16, ph), Buf()) for i in range(4)])
                gstep = [0]
                stq = []

                def sched(fn):
                    stq.append((gstep[0] + 1, fn))

                def run_due():
                    while stq and stq[0][0] <= gstep[0]:
                        stq.pop(0)[1]()
                ystg = Rot([(sb("ystg%d" % i, [128, 4, 128], BF16, ph), Buf()) for i in range(2)])
                sc_b = [psFb[0], psFb[0]]
                sc_aps = [psF[0][:, 0:128], psF[0][:, 0:128]]
                psU = [[(psF[2][:], psFb[2]), (psF[3][:], psFb[3])], [(psF[5][:], psFb[5]), (psT[1][:].bitcast(F32), psTb[1])]]
                psO = [(psF[1][:], psFb[1]), (psF[4][:], psFb[4])]
                tr_b = [psTb[0], psTb[0]]
                tr_aps = [psT[0][:, 0:512], psT[0][:, 0:512]]
                trot = Rot([0, 1])
                fo = [16, 17] + list(range(16))
                bo = [17, 16] + list(range(15, -1, -1))
                kf = {c: i for i, c in enumerate(fo)}
                kb = {c: i for i, c in enumerate(bo)}
                def load_raw(h):
                    P.dma("sp", qT[:], QKT_d[h * 256:(h + 1) * 256, :].rearrange("(c p) t -> p c t", p=128), writes=[rawb])
                    P.dma("sp", kT[:], QKT_d[2048 + h * 256:2048 + (h + 1) * 256, :].rearrange("(c p) t -> p c t", p=128),
                          writes=[rawb])
                    P.dma("sp", ktk[:], KTOK_d[:, h * 256:(h + 1) * 256].rearrange("(t p) n -> p t n", p=128), writes=[rawb])

                def load_sg(h):
                    P.dma("sp", sgt[:], SG_d[:, h * 512:(h + 1) * 512].rearrange("(t p) n -> p t n", p=128), writes=[sgb])

                def load_v(h):
                    vt_, vb_ = vtks.next()
                    P.dma("sp", vt_[:], V_d[:, h * 512:(h + 1) * 512].rearrange("(t p) n -> p t n", p=128), writes=[vb_])
                    return vt_, vb_

                load_raw(0)
                vnext = load_v(0)
                for h in range(8):
                    vtk, vb = vnext
                    want = lambda c: need_ctx or c < 16

                    def scale_ops(k):
                        r = {}
                        for d_, c in ((0, fo[k]), (1, bo[k])):
                            cs = slice(c * 128, (c + 1) * 128)
                            if k < NTILE - 1:
                                ks, ksb = ksr[d_].next()
                                P.op("pool", lambda e: e.tensor_scalar(out=ks[:], in0=ktk[:, c, :], scalar1=kdec[:, d_, h:h + 1],
                                                                        scalar2=0.0, op0=ALU.mult, op1=ALU.add),
                                     reads=[rawb, tb], writes=[ksb])
                                r["ks%d" % d_] = (ks, ksb)
                            if k > 0 and want(c):
                                qs, qsb = qsr[d_].next()
                                P.op("pool", lambda e: e.tensor_tensor(
                                    out=qs[:], in0=qT[:, :, cs], in1=qdec[:, d_, h, :].unsqueeze(1).to_broadcast([128, 2, 128]),
                                    op=ALU.mult), reads=[rawb, tb], writes=[qsb])
                                r["qs%d" % d_] = (qs, qsb)
                        return r

                    def finalize(ot, otb, c, h=h):
                        jk, jkb = junks.next()
                        P.op("act", lambda e: e.activation(out=jk[:], in_=ot[:], func=AF.Square), reads=[otb], writes=[jkb])

                        def st2():
                            ssq, ssqb = ssqs.next()
                            P.op("dve", lambda e: e.tensor_reduce(out=ssq[:], in_=jk[:], axis=AX.X, op=ALU.add), reads=[jkb],
                                 writes=[ssqb])
                            P.op("act", lambda e: e.activation(out=ssq[:], in_=ssq[:], func=AF.Sqrt, scale=1.0 / 512, bias=epsT[:]),
                                 reads=[ssqb, b_const], writes=[ssqb])

                            def st3():
                                P.op("dve", lambda e: e.reciprocal(out=ssq[:], in_=ssq[:]), reads=[ssqb], writes=[ssqb])
                                yt_, ytb = yts.next()
                                P.op("dve", lambda e: e.scalar_tensor_tensor(out=yt_[:], in0=ot[:], scalar=ssq[:, 0:1],
                                                                              in1=sgt[:, c, :], op0=ALU.mult, op1=ALU.mult),
                                     reads=[otb, ssqb, sgb], writes=[ytb])

                                def st4():
                                    ti = trot.next()
                                    P.tr(tr_b[ti], [(tr_aps[ti][:, e_ * 128:(e_ + 1) * 128], yt_[:, e_ * 128:(e_ + 1) * 128])
                                                    for e_ in range(4)], [ytb, b_const], identB[:])
                                    yg, ygb = ystg.next()
                                    P.op("act", lambda e: e.copy(out=yg[:], in_=tr_aps[ti].rearrange("p (h t) -> p h t", h=4)),
                                         reads=[tr_b[ti]], writes=[ygb])
                                    P.dma("sp", YT_d[h * 512:(h + 1) * 512, c * 128:(c + 1) * 128].rearrange("(e p) t -> p e t", p=128),
                                          yg[:], reads=[ygb])

                                sched(st4)

                            sched(st3)

                        sched(st2)

                    def deliver(c, ps_ap, ps_buf, dirn):
                        first = (kf[c] < kb[c]) if dirn == 0 else (kb[c] < kf[c])
                        if first:
                            P.op("act", lambda e: e.copy(out=part[:, c, :], in_=ps_ap), reads=[ps_buf], writes=[partb[c]])
                            return
                        ot, otb = ots.next()
                        other_absent = (dirn == 0 and kb[c] == 0)
                        if other_absent:
                            P.op("dve", lambda e: e.tensor_copy(out=ot[:], in_=ps_ap), reads=[ps_buf], writes=[otb])
                        else:
                            P.op("dve", lambda e: e.tensor_tensor(out=ot[:], in0=part[:, c, :], in1=ps_ap, op=ALU.add),
                                 reads=[ps_buf, partb[c]], writes=[otb])
                        finalize(ot, otb, c)

                    cur = [None, None]
                    sc = {0: scale_ops(0)}
                    for k in range(NTILE):
                        gstep[0] += 1
                        if k == 2 or (h == 0 and k == 0):
                            if not (h == 0 and k == 2):
                                load_sg(h)
                        if k + 1 < NTILE:
                            sc[k + 1] = scale_ops(k + 1)
                        r = sc.pop(k)
                        cf, cb_ = fo[k], bo[k]
                        csf = slice(cf * 128, (cf + 1) * 128)
                        if want(cf):
                            P.mm(sc_aps[k % 2], sc_b[k % 2], [(kT[:, dc, csf], qT[:, dc, csf]) for dc in range(2)], [rawb])
                        if k < NTILE - 1:
                            for d_, c in ((0, cf), (1, cb_)):
                                ks, ksb = r["ks%d" % d_]
                                for dc in range(2):
                                    ua, ub_ = psU[d_][dc]
                                    P.mm(ua, ub_, [(ks[:, dc * 128:(dc + 1) * 128], vtk[:, c, :])], [ksb, vb])
                        prev = [cur[0], cur[1]]
                        if want(cf):
                            pT, pTb = pTs.next()
                            P.op("dve", lambda e: e.tensor_tensor(out=pT[:], in0=sc_aps[k % 2], in1=Mall[:, h, :], op=ALU.mult),
                                 reads=[sc_b[k % 2], tb], writes=[pTb])
                        if k < NTILE - 1:
                            for d_ in range(2):
                                nxt = Sbf[d_].next()
                                for dc in range(2):
                                    ua, ub_ = psU[d_][dc]
                                    if k == 0:
                                        P.op("act", lambda e: e.copy(out=S[d_][:, dc, :], in_=ua), reads=[ub_], writes=[Sb_[d_][dc]])
                                    else:
                                        P.op("dve", lambda e: e.scalar_tensor_tensor(
                                            out=S[d_][:, dc, :], in0=S[d_][:, dc, :], scalar=cdec[:, d_, h:h + 1], in1=ua,
                                            op0=ALU.mult, op1=ALU.add), reads=[ub_, Sb_[d_][dc], tb], writes=[Sb_[d_][dc]])
                                    P.op("act", lambda e: e.copy(out=nxt[0][:, dc, :], in_=S[d_][:, dc, :]), reads=[Sb_[d_][dc]],
                                         writes=[nxt[1]])
                                cur[d_] = nxt
                        run_due()
                        if want(cf):
                            items = [(pT[:], vtk[:, cf, :])]
                            rds = [pTb, vb]
                            if k > 0:
                                qs, qsb = r["qs0"]
                                items += [(qs[:, dc, :], prev[0][0][:, dc, :]) for dc in range(2)]
                                rds += [qsb, prev[0][1]]
                            P.mm(psO[0][0], psO[0][1], items, rds)
                            deliver(cf, psO[0][0], psO[0][1], 0)
                        if want(cb_) and k > 0:
                            qs, qsb = r["qs1"]
                            P.mm(psO[1][0], psO[1][1], [(qs[:, dc, :], prev[1][0][:, dc, :]) for dc in range(2)], [qsb, prev[1][1]])
                            deliver(cb_, psO[1][0], psO[1][1], 1)
                        if k == 11 and h + 1 < 8:
                            vnext = load_v(h + 1)
                    if h + 1 < 8:
                        load_raw(h + 1)
                for _ in range(4):
                    gstep[0] += 1
                    run_due()
                assert not stq
                P.barrier()

        def phase_final(hsrc):
            with ExitStack() as ph:
                hin = [sb("fin%d" % i, [128, KC, 128], F32, ph) for i in range(2)]
                hinb = [Buf() for _ in range(2)]
                stg = [sb("fstg%d" % i, [128, D], F32, ph) for i in range(2)]
                stgb = [Buf() for _ in range(2)]
                prot = Rot(list(range(4)))
                for tt in range(16):
                    hi, hb = hin[tt % 2], hinb[tt % 2]
                    sg_, sgb = stg[tt % 2], stgb[tt % 2]
                    P.dma("sp", hi[:], hsrc[:, tt * 128:(tt + 1) * 128].rearrange("(c p) t -> p c t", p=128), writes=[hb])
                    for cg in range(4):
                        pi = prot.next()
                        P.tr(psFb[pi], [(psF[pi][:, i * 128:(i + 1) * 128], hi[:, cg * 4 + i, :]) for i in range(4)], [hb, b_const],
                             identF[:])
                        if cg % 2 == 0:
                            P.op("act", lambda e, pi=pi, cg=cg, sg_=sg_: e.copy(out=sg_[:, cg * 512:(cg + 1) * 512], in_=psF[pi][:]),
                                 reads=[psFb[pi]], writes=[sgb])
                        else:
                            P.op("dve", lambda e, pi=pi, cg=cg, sg_=sg_: e.tensor_copy(out=sg_[:, cg * 512:(cg + 1) * 512], in_=psF[pi][:]),
                                 reads=[psFb[pi]], writes=[sgb])
                    P.dma("pool", y_d[tt * 128:(tt + 1) * 128, :], sg_[:], reads=[sgb])
                P.barrier()

        def dump(hsrc):
            with ExitStack() as ph:
                t = sb("dmp", [128, KC, 576], F32, ph)
                tb_ = Buf()
                for i in range(4):
                    P.dma("sp", t[:], hsrc[:, i * 576:(i + 1) * 576].rearrange("(c p) t -> p c t", p=128), writes=[tb_])
                    P.dma("sp", dbg_d[:, i * 576:(i + 1) * 576].rearrange("(c p) t -> p c t", p=128), t[:], reads=[tb_])
                P.barrier()

        def program():
            if done("init"):
                phase_init()
                return hA
            phase_init(phase_mod)
            for l in range(DEPTH):
                need_ctx = l < DEPTH - 1
                j2 = l // 2
                if l % 2 == 0:
                    phase_proj_even(l, j2, need_ctx)
                    phase_fourier(l, need_ctx, lambda l=l, need_ctx=need_ctx: phase_att(l, need_ctx))
                    phase_out(l, woute_d[j2], need_ctx)
                else:
                    phase_proj_odd(l, j2, need_ctx)
                    phase_ret(l, j2, need_ctx)
                    phase_out(l, wouto_d[j2], need_ctx)
                if done("mix%d" % l):
                    return hB
                phase_ffn(l, need_ctx)
                if done("ffn%d" % l):
                    return hA
            return hA

        hfin = program()
        if dump_h:
            dump(hfin)
        phase_final(hfin)
        print("instructions:", P.nins, {k: e.cnt for k, e in P.E.items()})
    return nc


def _rope_table(hd_axis):
    nf = hd_axis // 2
    inv = (10000.0 ** (-np.arange(0, hd_axis, 2, dtype=np.float32) / hd_axis)).astype(np.float32)
    t = np.arange(NL)
    row = (t // 64).astype(np.float32)
    col = (t % 64).astype(np.float32)
    cos_parts, sin_parts = [], []
    for pos in (row, col):
        ang = (pos[:, None] * inv[None, :]).astype(np.float32)
        c, s = np.cos(ang).astype(np.float32), np.sin(ang).astype(np.float32)
        cos_parts += [c, c]
        sin_parts += [-s, s]
    return np.ascontiguousarray(np.concatenate(cos_parts + sin_parts, axis=1).astype(np.float32))


def _consts():
    c = {}
    c["ropeE"] = _rope_table(64)
    c["ropeO"] = _rope_table(128)
    t = np.arange(NL, dtype=np.int64)
    m = (t[:, None] * t[None, :]) % NL
    ang = 2 * np.pi * m / NL
    c["dftc"] = np.cos(ang).astype(np.float32)
    c["dfts"] = np.sin(ang).astype(np.float32)
    t = np.arange(NX, dtype=np.int64)
    ang = 2 * np.pi * ((t[:, None] * t[None, :]) % NX) / NX
    c["dftc256"] = np.cos(ang).astype(np.float32)
    c["dfts256"] = np.sin(ang).astype(np.float32)
    t = np.arange(128, dtype=np.int64)
    ang = 2 * np.pi * ((t[:, None] * t[None, :]) % 128) / 128
    ch = np.concatenate([np.cos(ang), -np.sin(ang)], axis=1)
    c["dftch"] = (ch / np.sqrt(NL * 128.0)).astype(np.float32)
    c["dftchx"] = (ch / np.sqrt(NX * 128.0)).astype(np.float32)
    j = np.arange(128, dtype=np.float32)[:, None]
    i = np.arange(128, dtype=np.float32)[None, :]
    rett = np.zeros((128, 6, 128), np.float32)
    rett[:, 0] = np.maximum(i - j, 0)
    rett[:, 1] = (i >= j) / 16.0
    rett[:, 2] = np.maximum(j - i, 0)
    rett[:, 3] = (j > i) / 16.0
    rett[:, 4] = np.broadcast_to(i + 1, (128, 128))
    rett[:, 5] = np.broadcast_to(128 - i, (128, 128))
    c["rett"] = rett
    retc = np.zeros((128, 2), np.float32)
    retc[:, 0] = 127 - np.arange(128)
    retc[:, 1] = np.arange(128)
    c["retc"] = retc
    return c


_NC_CACHE = {}


def _prep_inputs(inp):
    f = lambda a: np.ascontiguousarray(np.asarray(a, dtype=np.float32))
    shared = dict(_consts())
    shared["w_mod"] = f(inp["w_mod"])
    shared["bmodT"] = f(np.asarray(inp["b_mod"]).reshape(DEPTH, 96, 128).transpose(2, 0, 1))
    shared["w_in_even"] = f(inp["w_in_even"])
    shared["w_out_even"] = f(inp["w_out_even"])
    shared["qg_rep"] = f(np.tile(np.asarray(inp["q_gain_even"])[:, None, :], (1, 128, 4)))
    shared["kg_rep"] = f(np.tile(np.asarray(inp["k_gain_even"])[:, None, :], (1, 128, 4)))
    shared["w_in_odd"] = f(inp["w_in_odd"])
    shared["w_out_odd"] = f(inp["w_out_odd"])
    shared["lgf_rep"] = f(np.tile(np.asarray(inp["log_decay_fwd"])[:, None, :], (1, 128, 1)))
    shared["lgb_rep"] = f(np.tile(np.asarray(inp["log_decay_bwd"])[:, None, :], (1, 128, 1)))
    shared["w_ffn_in"] = f(inp["w_ffn_in"])
    shared["w_ffn_out"] = f(inp["w_ffn_out"])
    shared["cxT"] = f(np.asarray(inp["c_ctx"]).reshape(KC, 128).T)
    x = np.asarray(inp["x"])
    c = np.asarray(inp["c"])
    ctx = np.asarray(inp["ctx"])
    maps = []
    for b in range(NCORES):
        m = dict(shared)
        m["x"] = f(x[b])
        m["ctx"] = f(ctx[b])
        m["cT"] = f(c[b].reshape(KC, 128).T)
        maps.append(m)
    return maps


def kernel(**inputs):
    if "nc" not in _NC_CACHE:
        _NC_CACHE["nc"] = build()
    nc = _NC_CACHE["nc"]
    maps = _prep_inputs(inputs)
    res = run_bass_kernel_spmd(nc, maps, core_ids=list(range(NCORES)))
    return np.stack([np.asarray(r["y"], dtype=np.float32) for r in res.results], axis=0)
```

```python
import numpy as np
from contextlib import ExitStack
import concourse.bass as bass
import concourse.mybir as mybir
from concourse.bass_utils import run_bass_kernel_spmd

F32 = mybir.dt.float32
BF16 = mybir.dt.bfloat16
AF = mybir.ActivationFunctionType
ALU = mybir.AluOpType
AX = mybir.AxisListType

NCORES = 8
D = 2048
KC = 16
NL = 2048
NX = 256
NT = NL + NX
NTILE = NT // 128
FF = 5632
FC = FF // 128
DEPTH = 4
EPS = 1e-6
NDS = 12

SBLOCKS = [
    (0, [(0, 512, 0), (512, 256, 0)]),
    (768, [(0, 512, 0), (512, 256, 0)]),
    (1536, [(0, 512, 0), (512, 256, 1)]),
]


class Buf:
    __slots__ = ("w", "r", "name")

    def __init__(self, name=""):
        self.w = {}
        self.r = {}
        self.name = name


class Eng:
    def __init__(self, e, sem, inorder=False):
        self.e = e
        self.sem = sem
        self.cnt = 0
        self.seen = {}
        self.inorder = inorder


class Prog:
    def __init__(self, nc, st):
        self.nc = nc
        mk = lambda n: st.enter_context(nc.semaphore(n))
        self.E = {
            "pe": Eng(nc.tensor, mk("s_pe"), True),
            "act": Eng(nc.scalar, mk("s_act")),
            "dve": Eng(nc.vector, mk("s_dve")),
            "pool": Eng(nc.gpsimd, mk("s_pool")),
            "sp": Eng(nc.sync, mk("s_sp")),
        }
        self.dsem = {q: [[mk("d_%s%d" % (q, i)), 0] for i in range(NDS)] for q in ("sp", "pool", "act")}
        self.dptr = {"sp": 0, "pool": 0, "act": 0}
        self.nins = 0

    def _need(self, E, reads, writes):
        need = {}

        def mg(d):
            for k, t in d.items():
                if k not in need or need[k][1] < t[1]:
                    need[k] = t

        for b in reads:
            mg(b.w)
        for b in writes:
            mg(b.w)
            mg(b.r)
        for k, (sem, v) in need.items():
            if E.inorder and sem is E.sem:
                continue
            if E.seen.get(k, 0) >= v:
                continue
            E.e.wait_ge(sem, v)
            E.seen[k] = v

    def _mark(self, t, reads, writes):
        k = id(t[0])
        for b in reads:
            if k not in b.r or b.r[k][1] < t[1]:
                b.r[k] = t
        for b in writes:
            b.w = {k: t}
            b.r = {}

    def op(self, en, fn, reads=(), writes=()):
        E = self.E[en]
        self._need(E, reads, writes)
        ins = fn(E.e)
        E.cnt += 1
        ins.then_inc(E.sem, 1)
        self._mark((E.sem, E.cnt), reads, writes)
        self.nins += 1

    def mm(self, out_ap, outbuf, items, reads, start=True, stop=True):
        E = self.E["pe"]
        self._need(E, reads, [outbuf])
        n = len(items)
        ins = None
        for i, (l, r) in enumerate(items):
            ins = E.e.matmul(out_ap, lhsT=l, rhs=r, start=(start and i == 0), stop=(stop and i == n - 1))
        E.cnt += 1
        ins.then_inc(E.sem, 1)
        self._mark((E.sem, E.cnt), reads, [outbuf])
        self.nins += n

    def tr(self, outbuf, items, reads, ident):
        E = self.E["pe"]
        self._need(E, reads, [outbuf])
        ins = None
        for o, i in items:
            ins = E.e.transpose(out=o, in_=i, identity=ident)
        E.cnt += 1
        ins.then_inc(E.sem, 1)
        self._mark((E.sem, E.cnt), reads, [outbuf])
        self.nins += len(items)

    def dma(self, q, out, in_, reads=(), writes=(), **kw):
        E = self.E[q]
        self._need(E, reads, writes)
        lst = self.dsem[q]
        i = self.dptr[q]
        self.dptr[q] = (i + 1) % len(lst)
        sem, cnt = lst[i]
        k = id(sem)
        if cnt > 0 and E.seen.get(k, 0) < cnt:
            E.e.wait_ge(sem, cnt)
            E.seen[k] = cnt
        ins = E.e.dma_start(out=out, in_=in_, **kw)
        ins.then_inc(sem, 16)
        lst[i][1] = cnt + 16
        self._mark((sem, cnt + 16), reads, writes)
        self.nins += 1

    def barrier(self):
        for E in self.E.values():
            for E2 in self.E.values():
                if E2 is E or E2.cnt == 0:
                    continue
                k = id(E2.sem)
                if E.seen.get(k, 0) < E2.cnt:
                    E.e.wait_ge(E2.sem, E2.cnt)
                    E.seen[k] = E2.cnt
            for q in self.dsem:
                for sem, cnt in self.dsem[q]:
                    k = id(sem)
                    if cnt > 0 and E.seen.get(k, 0) < cnt:
                        E.e.wait_ge(sem, cnt)
                        E.seen[k] = cnt


class Rot:
    def __init__(self, items):
        self.items = items
        self.i = 0

    def next(self):
        x = self.items[self.i]
        self.i = (self.i + 1) % len(self.items)
        return x


def build(stop_after=None, dump_h=False):
    nc = bass.Bass("TRN2", target_bir_lowering=False)
    dt_in = lambda name, shape: nc.dram_tensor(name, shape, F32, kind="ExternalInput").ap()
    x_d = dt_in("x", [NL, D])
    ctx_d = dt_in("ctx", [NX, D])
    cT_d = dt_in("cT", [128, KC])
    cxT_d = dt_in("cxT", [128, KC])
    wmod_d = dt_in("w_mod", [DEPTH, D, 6 * D])
    bmodT_d = dt_in("bmodT", [128, DEPTH, 96])
    wine_d = dt_in("w_in_even", [2, D, 3072])
    woute_d = dt_in("w_out_even", [2, D, D])
    qg_d = dt_in("qg_rep", [2, 128, 512])
    kg_d = dt_in("kg_rep", [2, 128, 512])
    wino_d = dt_in("w_in_odd", [2, D, 12288])
    wouto_d = dt_in("w_out_odd", [2, 4096, D])
    lgf_d = dt_in("lgf_rep", [2, 128, 8])
    lgb_d = dt_in("lgb_rep", [2, 128, 8])
    wfi_d = dt_in("w_ffn_in", [DEPTH, D, 2 * FF])
    wfo_d = dt_in("w_ffn_out", [DEPTH, FF, D])
    ropeE_d = dt_in("ropeE", [NL, 256])
    ropeO_d = dt_in("ropeO", [NL, 512])
    dftc_d = dt_in("dftc", [NL, NL])
    dfts_d = dt_in("dfts", [NL, NL])
    dftc256_d = dt_in("dftc256", [NX, NX])
    dfts256_d = dt_in("dfts256", [NX, NX])
    dftch_d = dt_in("dftch", [128, 256])
    dftchx_d = dt_in("dftchx", [128, 256])
    rett_d = dt_in("rett", [128, 6, 128])
    retc_d = dt_in("retc", [128, 2])
    y_d = nc.dram_tensor("y", [NL, D], F32, kind="ExternalOutput").ap()
    hA = nc.dram_tensor("hA", [D, NT], F32).ap()
    hB = nc.dram_tensor("hB", [D, NT], F32).ap()
    QKT_d = nc.dram_tensor("QKT", [4096, NT], BF16).ap()
    KTOK_d = nc.dram_tensor("KTOK", [NT, 2048], BF16).ap()
    V_d = nc.dram_tensor("VTOK", [NT, 4096], BF16).ap()
    SG_d = nc.dram_tensor("SGTOK", [NT, 4096], BF16).ap()
    F_d = nc.dram_tensor("FTOK", [NT, 512], BF16).ap()
    YT_d = nc.dram_tensor("YT", [4096, NT], BF16).ap()
    if dump_h:
        dbg_d = nc.dram_tensor("dbg", [D, NT], F32, kind="ExternalOutput").ap()

    st = ExitStack()
    with st:
        P = Prog(nc, st)
        _uid = [0]

        def sb(name, shape, dt, s=st):
            _uid[0] += 1
            return s.enter_context(nc.sbuf_tensor("sb%d_%s" % (_uid[0], name), shape, dt))
        identF = sb("identF", [128, 128], F32)
        identB = sb("identB", [128, 128], BF16)
        onesB = sb("onesB", [128, 128], BF16)
        onesF = sb("onesF", [128, 128], F32)
        modsb = sb("modsb", [128, DEPTH, 96, 2], F32)
        epsT = sb("epsT", [128, 1], F32)
        b_const = Buf("const")
        b_modl = [Buf("mod%d" % i) for i in range(DEPTH)]
        condT = sb("condT", [128, KC, 2], BF16)
        bT = sb("bT", [128, DEPTH, 96], F32)
        cb = Buf("cond")
        psF = [st.enter_context(nc.psum_tensor("psF%d" % i, [128, 512], F32)) for i in range(6)]
        psFb = [Buf("psF%d" % i) for i in range(6)]
        psT = [st.enter_context(nc.psum_tensor("psT%d" % i, [128, 1024], BF16)) for i in range(2)]
        psTb = [Buf("psT%d" % i) for i in range(2)]

        P.op("pool", lambda e: e.memset(identF[:], 0.0), writes=[b_const])
        P.op("pool", lambda e: e.affine_select(out=identF[:], in_=identF[:], pattern=[[-1, 128]], base=0,
                                                channel_multiplier=1, compare_op=ALU.not_equal, fill=1.0),
             writes=[b_const])
        P.op("dve", lambda e: e.tensor_copy(out=identB[:], in_=identF[:]), reads=[b_const], writes=[b_const])
        P.op("dve", lambda e: e.memset(onesB[:], 1.0), writes=[b_const])
        P.op("dve", lambda e: e.memset(onesF[:], 1.0), writes=[b_const])
        P.op("dve", lambda e: e.memset(epsT[:], EPS), writes=[b_const])

        def done(tag):
            return stop_after is not None and stop_after == tag

        def phase_init(inner=None):
            with ExitStack() as ph:
                xin = [sb("xin%d" % i, [128, D], F32, ph) for i in range(2)]
                xinb = [Buf() for _ in range(2)]
                stg = [sb("xstg%d" % i, [128, KC, 128], F32, ph) for i in range(2)]
                stgb = [Buf() for _ in range(2)]
                prot = Rot([0, 1])

                def tiles():
                  for tt in range(NTILE):
                    yield from tile(tt)

                def tile(tt):
                    src = x_d[tt * 128:(tt + 1) * 128, :] if tt < 16 else ctx_d[(tt - 16) * 128:(tt - 15) * 128, :]
                    xi, xb = xin[tt % 2], xinb[tt % 2]
                    sg_, sgb = stg[tt % 2], stgb[tt % 2]
                    P.dma("sp", xi[:], src, writes=[xb])
                    for cg in range(4):
                        pi = prot.next()
                        P.tr(psFb[pi], [(psF[pi][:, i * 128:(i + 1) * 128], xi[:, (cg * 4 + i) * 128:(cg * 4 + i + 1) * 128])
                                        for i in range(4)], [xb, b_const], identF[:])
                        en = "act" if cg % 2 == 0 else "dve"
                        dst = sg_[:, cg * 4:(cg + 1) * 4, :]
                        srcp = psF[pi][:].rearrange("p (c t) -> p c t", c=4)
                        if en == "act":
                            P.op("act", lambda e, d=dst, s=srcp: e.copy(out=d, in_=s), reads=[psFb[pi]], writes=[sgb])
                        else:
                            P.op("dve", lambda e, d=dst, s=srcp: e.tensor_copy(out=d, in_=s), reads=[psFb[pi]], writes=[sgb])
                    P.dma("sp", hA[:, tt * 128:(tt + 1) * 128].rearrange("(c p) t -> p c t", p=128), sg_[:], reads=[sgb])
                    yield

                gen = tiles()
                if inner is not None:
                    inner(gen)
                for _ in gen:
                    pass
                P.barrier()

        def load_w(wt, wb, w_ap, col0, ncols, dstcol=0):
            K = w_ap.shape[0]
            P.dma("pool", wt[:, 0:K // 128, dstcol:dstcol + ncols],
                  w_ap[:, col0:col0 + ncols].rearrange("(c p) n -> p c n", p=128), writes=[wb])

        def mod_slab(l, s_, wt, wb, pbank, ncols=512, do_load=True, do_compute=True):
            wv = wt[:, 0:KC * ncols].rearrange("p (c n) -> p c n", n=ncols)
            if do_load:
                P.dma("pool", wv, wmod_d[l][:, s_ * ncols:(s_ + 1) * ncols].rearrange("(c p) n -> p c n", p=128), writes=[wb])
            if not do_compute:
                return
            for j in range(ncols // 128):
                n = s_ * (ncols // 128) + j
                pt_, pb_ = pbank()
                P.mm(pt_, pb_, [(wv[:, kc, j * 128:(j + 1) * 128], condT[:, kc, :]) for kc in range(KC)], [wb, cb])
                P.op("dve", lambda e: e.tensor_scalar(out=modsb[:, l, n, :], in0=pt_, scalar1=bT[:, l, n:n + 1], scalar2=None,
                                                       op0=ALU.add), reads=[pb_, cb], writes=[b_modl[l]])
            if (s_ + 1) * ncols == 6 * D:
                for slot in (1, 4):
                    P.op("dve", lambda e: e.tensor_scalar(
                        out=modsb[:, l, slot * 16:(slot + 1) * 16, :], in0=modsb[:, l, slot * 16:(slot + 1) * 16, :],
                        scalar1=1.0, scalar2=None, op0=ALU.add), reads=[b_modl[l]], writes=[b_modl[l]])

        def phase_mod(gen=None):
            with ExitStack() as ph:
                cT = sb("cT", [128, 2, KC], F32, ph)
                wts = [sb("wm%d" % i, [128, KC * 512], BF16, ph) for i in range(2)]
                wbs = [Buf() for _ in range(2)]
                P.dma("sp", cT[:, 0, :], cT_d, writes=[cb])
                P.dma("sp", cT[:, 1, :], cxT_d, writes=[cb])
                P.dma("sp", bT[:], bmodT_d, writes=[cb])
                P.op("act", lambda e: e.activation(out=condT[:].rearrange("p k j -> p j k"), in_=cT[:], func=AF.Silu), reads=[cb],
                     writes=[cb])
                prot = Rot([2, 3])

                def pbank():
                    pi = prot.next()
                    return psF[pi][:, 0:2], psFb[pi]

                for s_ in range(24):
                    mod_slab(0, s_, wts[s_ % 2], wbs[s_ % 2], pbank)
                    if gen is not None:
                        next(gen, None)
                if gen is not None:
                    for _ in gen:
                        pass
                P.barrier()

        bg_list = [(l, s_) for l in range(1, DEPTH) for s_ in range(48)]
        bg_pos = [0]
        bg_rot = Rot([0, 1])
        bg_only1 = [False]

        def bg_bank():
            ti = 1 if bg_only1[0] else bg_rot.next()
            return psT[ti][:].bitcast(F32)[:, 0:2], psTb[ti]

        bg_loaded = []

        def bg_load(wt, wb, maxlayer):
            if bg_pos[0] >= len(bg_list) or bg_list[bg_pos[0]][0] > maxlayer:
                return False
            l, s_ = bg_list[bg_pos[0]]
            bg_pos[0] += 1
            mod_slab(l, s_, wt, wb, bg_bank, 256, True, False)
            bg_loaded.append((l, s_, wt, wb))
            return True

        def bg_compute():
            if not bg_loaded:
                return False
            l, s_, wt, wb = bg_loaded.pop(0)
            mod_slab(l, s_, wt, wb, bg_bank, 256, False, True)
            return True

        def bg_step(wt, wb, upto_layer=None):
            if bg_pos[0] >= len(bg_list):
                return False
            l, s_ = bg_list[bg_pos[0]]
            bg_pos[0] += 1
            mod_slab(l, s_, wt, wb, bg_bank, 256)
            return True

        class ModRes:
            def __init__(self, ph, tag, nh=2):
                self.hin = [sb("hin%s%d" % (tag, i), [128, KC, 256], F32, ph) for i in range(nh)]
                self.hinb = [Buf() for _ in range(nh)]
                self.sq = sb("sq" + tag, [128, KC, 256], BF16, ph)
                self.sqb = Buf()
                self.sd = sb("sd" + tag, [128, 256], F32, ph)
                self.sdb = Buf()
                self.i = 0

        def modulate(mr, hsrc, tok0, l, slot_shift, slot_scale, j, uT, ub, off, pi):
            n = 256
            hi, hb = mr.hin[mr.i % len(mr.hin)], mr.hinb[mr.i % len(mr.hin)]
            mr.i += 1
            P.dma("sp", hi[:], hsrc[:, tok0:tok0 + n].rearrange("(c p) t -> p c t", p=128), writes=[hb])
            P.op("act", lambda e: e.activation(out=mr.sq[:], in_=hi[:], func=AF.Square), reads=[hb], writes=[mr.sqb])
            yield
            P.mm(psF[pi][:, 0:n], psFb[pi], [(onesB[:], mr.sq[:, c, :]) for c in range(KC)], [mr.sqb, b_const])
            P.op("act", lambda e: e.activation(out=mr.sd[:], in_=psF[pi][:, 0:n], func=AF.Sqrt, scale=1.0 / D, bias=epsT[:]),
                 reads=[psFb[pi], b_const], writes=[mr.sdb])
            P.op("dve", lambda e: e.reciprocal(out=mr.sd[:], in_=mr.sd[:]), reads=[mr.sdb], writes=[mr.sdb])
            P.op("dve", lambda e: e.tensor_tensor(out=hi[:], in0=hi[:], in1=mr.sd[:].unsqueeze(1).to_broadcast([128, KC, n]),
                                                   op=ALU.mult), reads=[hb, mr.sdb], writes=[hb])
            for c in range(KC):
                P.op("act", lambda e, c=c: e.activation(
                    out=uT[:, c, off:off + n], in_=hi[:, c, :], func=AF.Identity,
                    scale=modsb[:, l, slot_scale * 16 + c, j:j + 1], bias=modsb[:, l, slot_shift * 16 + c, j:j + 1]),
                    reads=[hb, b_modl[l]], writes=[ub])

        def modulate_gen(mr, hsrc, sbi, l, slot_shift, slot_scale, uT, ub):
            tok0, _ = SBLOCKS[sbi]
            for u in range(3):
                g0 = tok0 + u * 256
                for _ in modulate(mr, hsrc, g0, l, slot_shift, slot_scale, 1 if g0 >= NL else 0, uT, ub, u * 256, 4 + (u % 2)):
                    yield
                yield

        def modulate_block(mr, hsrc, sbi, l, slot_shift, slot_scale, uT, ub):
            for _ in modulate_gen(mr, hsrc, sbi, l, slot_shift, slot_scale, uT, ub):
                pass

        def linear_residual(ph, w_ap, src, srcb, tok0, pieces, hsrc, hdst, l, gslot, need_ctx, wts, wbs, wctr, hook=None,
                            store_q="sp"):
            K = w_ap.shape[0]
            kc = K // 128
            ncols = 256
            ht = ph["ht"]
            for s in range(D // ncols):
                wt, wb = wts[wctr[0] % 2], wbs[wctr[0] % 2]
                wctr[0] += 1
                wv = wt[:, 0:kc * ncols].rearrange("p (c n) -> p c n", n=ncols)
                P.dma("pool", wv, w_ap[:, s * ncols:(s + 1) * ncols].rearrange("(c p) n -> p c n", p=128), writes=[wb])
                for jj in range(ncols // 128):
                    nch = s * (ncols // 128) + jj
                    for (o, n, isctx) in pieces:
                        if isctx and not need_ctx:
                            continue
                        pi = ph["prot"].next()
                        g0 = tok0 + o
                        hi_t, hi_b, ho_t, ho_b = ht.next()
                        P.dma("sp", hi_t[:, 0:n], hsrc[nch * 128:(nch + 1) * 128, g0:g0 + n], writes=[hi_b])
                        P.mm(psF[pi][:, 0:n], psFb[pi],
                             [(wv[:, c, jj * 128:(jj + 1) * 128], src[:, c, o:o + n]) for c in range(kc)],
                             [wb, srcb[o] if isinstance(srcb, dict) else srcb])
                        P.op("dve", lambda e, pi=pi, n=n, nch=nch, isctx=isctx, hi_t=hi_t, ho_t=ho_t: e.scalar_tensor_tensor(
                            out=ho_t[:, 0:n], in0=psF[pi][:, 0:n], scalar=modsb[:, l, gslot * 16 + nch, isctx:isctx + 1],
                            in1=hi_t[:, 0:n], op0=ALU.mult, op1=ALU.add), reads=[psFb[pi], hi_b, b_modl[l]], writes=[ho_b])
                        P.dma(store_q, hdst[nch * 128:(nch + 1) * 128, g0:g0 + n], ho_t[:, 0:n], reads=[ho_b])
                    if hook is not None:
                        hook()

        def mk_ht(ph_stack, tag, n=3):
            items = []
            for i in range(n):
                items.append((sb("hi%s%d" % (tag, i), [128, 512], F32, ph_stack), Buf(),
                              sb("ho%s%d" % (tag, i), [128, 512], F32, ph_stack), Buf()))
            return Rot(items)

        def phase_ffn(l, need_ctx):
            with ExitStack() as ph:
                actT = sb("actT", [128, FC, 768], BF16, ph)
                actb = Buf()
                uT = sb("uTf", [128, KC, 768], BF16, ph)
                ub = Buf()
                mr = ModRes(ph, "f", 1)
                wts = [sb("wf%d" % i, [128, FC * 256], BF16, ph) for i in range(2)]
                wbs = [Buf() for _ in range(2)]
                wctr = [0]
                sgs = Rot([(sb("sgf%d" % i, [128, 512], F32, ph), Buf()) for i in range(2)])
                phd = {"ht": mk_ht(ph, "f", 4), "prot": Rot(list(range(4)))}
                prot = Rot([(0, 1), (2, 3)])

                def nextw():
                    i = wctr[0] % 2
                    wctr[0] += 1
                    return wts[i], wbs[i]

                modulate_block(mr, hB, 0, l, 3, 4, uT, ub)
                for sbi, (tok0, pieces) in enumerate(SBLOCKS):
                    for s_ in range(FC // 2):
                        wt, wb = nextw()
                        wv = wt[:, 0:KC * 512].rearrange("p (c n) -> p c n", n=512)
                        P.dma("pool", wv[:, :, 0:256], wfi_d[l][:, s_ * 256:(s_ + 1) * 256].rearrange("(c p) n -> p c n", p=128),
                              writes=[wb])
                        P.dma("pool", wv[:, :, 256:512],
                              wfi_d[l][:, FF + s_ * 256:FF + (s_ + 1) * 256].rearrange("(c p) n -> p c n", p=128), writes=[wb])
                        for jj in range(2):
                            j = 2 * s_ + jj
                            for (o, n, isctx) in pieces:
                                if isctx and not need_ctx:
                                    continue
                                pg, pu = prot.next()
                                P.mm(psF[pg][:, 0:n], psFb[pg],
                                     [(wv[:, c, jj * 128:(jj + 1) * 128], uT[:, c, o:o + n]) for c in range(KC)], [wb, ub])
                                P.mm(psF[pu][:, 0:n], psFb[pu],
                                     [(wv[:, c, 256 + jj * 128:256 + (jj + 1) * 128], uT[:, c, o:o + n]) for c in range(KC)],
                                     [wb, ub])
                                sgt, sgb = sgs.next()
                                P.op("act", lambda e: e.activation(out=sgt[:, 0:n], in_=psF[pg][:, 0:n], func=AF.Silu),
                                     reads=[psFb[pg]], writes=[sgb])
                                P.op("dve", lambda e: e.tensor_tensor(out=actT[:, j, o:o + n], in0=sgt[:, 0:n], in1=psF[pu][:, 0:n],
                                                                       op=ALU.mult), reads=[psFb[pu], sgb], writes=[actb])
                    hook = None
                    if sbi + 1 < len(SBLOCKS):
                        gen = modulate_gen(mr, hB, sbi + 1, l, 3, 4, uT, ub)
                        hook = lambda gen=gen: next(gen, None)
                    linear_residual(phd, wfo_d[l], actT, actb, tok0, pieces, hB, hA, l, 5, need_ctx, wts, wbs, wctr, hook)
                    if hook is not None:
                        for _ in gen:
                            pass
                P.barrier()

        def phase_out(l, w_ap, need_ctx):
            K = w_ap.shape[0]
            kc = K // 128
            with ExitStack() as ph:
                yT = sb("yTo", [128, kc, NT], BF16, ph)
                pieces = [(0, 512, 0), (512, 512, 0), (1024, 512, 0), (1536, 512, 0), (2048, 256, 1)]
                ybs = {}
                for (o, n, isctx) in pieces:
                    if isctx and not need_ctx:
                        continue
                    ybs[o] = Buf()
                    for c0 in range(0, kc, 16):
                        P.dma("sp", yT[:, c0:c0 + 16, o:o + n],
                              YT_d[c0 * 128:(c0 + 16) * 128, o:o + n].rearrange("(c p) t -> p c t", p=128), writes=[ybs[o]])
                wts = [sb("wo%d" % i, [128, kc * 256], BF16, ph) for i in range(2)]
                wbs = [Buf() for _ in range(2)]
                phd = {"ht": mk_ht(ph, "o", 6 if kc == 16 else 4), "prot": Rot(list(range(4)))}
                linear_residual(phd, w_ap, yT, ybs, 0, pieces, hA, hB, l, 2, need_ctx, wts, wbs, [0], None, "act")
                if not need_ctx:
                    pass
                P.barrier()

        def rope_tm(src_ap, src_buf, tab, tabb, gt, nh, hd, t1, t1b, t2, t2b, out_ap, outb):
            q = hd // 4
            cosb = tab[:, gt, 0:hd].unsqueeze(1).to_broadcast([128, nh, hd])
            P.op("dve", lambda e: e.tensor_tensor(out=t1[:, 0:nh * hd].rearrange("p (h d) -> p h d", h=nh),
                                                   in0=src_ap.rearrange("p (h d) -> p h d", h=nh), in1=cosb, op=ALU.mult),
                 reads=[src_buf, tabb], writes=[t1b])
            s5 = src_ap.rearrange("p (h a j d) -> p h a j d", h=nh, a=2, j=2)
            o5 = t2[:, 0:nh * hd].rearrange("p (h a j d) -> p h a j d", h=nh, a=2, j=2)
            sn = tab[:, gt, hd:2 * hd].rearrange("p (a j d) -> p a j d", a=2, j=2)
            for jx in range(2):
                sinb = sn[:, :, jx, :].unsqueeze(1).to_broadcast([128, nh, 2, q])
                P.op("dve", lambda e, jx=jx, sinb=sinb: e.tensor_tensor(out=o5[:, :, :, jx, :], in0=s5[:, :, :, 1 - jx, :],
                                                                       in1=sinb, op=ALU.mult),
                     reads=[src_buf, tabb], writes=[t2b])
            P.op("dve", lambda e: e.tensor_tensor(out=out_ap, in0=t1[:, 0:nh * hd], in1=t2[:, 0:nh * hd], op=ALU.add),
                 reads=[t1b, t2b], writes=[outb])

        def phase_proj_even(l, j2, need_ctx):
            with ExitStack() as ph:
                tab = sb("ropeE", [128, 16, 256], F32, ph)
                tabb = Buf()
                P.dma("sp", tab[:], ropeE_d.rearrange("(t p) n -> p t n", p=128), writes=[tabb])
                gq = sb("gq", [128, 512], F32, ph)
                gk = sb("gk", [128, 512], F32, ph)
                gb = Buf()
                P.dma("sp", gq[:], qg_d[j2], writes=[gb])
                P.dma("sp", gk[:], kg_d[j2], writes=[gb])
                uTs = [(sb("uTe%d" % i, [128, KC, 768], BF16, ph), Buf()) for i in range(2)]
                mr = ModRes(ph, "e", 1)
                wts = [sb("we%d" % i, [128, KC, 512], BF16, ph) for i in range(2)]
                wbs = [Buf() for _ in range(2)]
                qkst = sb("qkst", [128, 16, 768], BF16, ph)
                qkstb = Buf()
                tokst = [sb("tokst%d" % i, [128, 6, 512], BF16, ph) for i in range(2)]
                tokstb = [Buf() for _ in range(2)]
                tmps = Rot([(sb("sqe%d" % i, [128, 512], F32, ph), Buf(), sb("ssqe%d" % i, [128, 4], F32, ph), Buf(),
                             sb("qne%d" % i, [128, 512], F32, ph), Buf(), sb("t1e%d" % i, [128, 512], F32, ph), Buf(),
                             sb("t2e%d" % i, [128, 512], F32, ph), Buf()) for i in range(3)])
                qtoks = Rot([(sb("qtoke%d" % i, [128, 512], BF16, ph), Buf()) for i in range(4)])
                prot = Rot(list(range(4)))
                trot = Rot([0, 1])
                wi = 0
                pend = []

                def flush(keep):
                    while len(pend) > keep:
                        pend.pop(0)()

                modulate_block(mr, hA, 0, l, 0, 1, uTs[0][0], uTs[0][1])
                for sbi, (tok0, pieces) in enumerate(SBLOCKS):
                    uT, ub = uTs[sbi % 2]
                    gen = None
                    if sbi + 1 < len(SBLOCKS):
                        gen = modulate_gen(mr, hA, sbi + 1, l, 0, 1, uTs[(sbi + 1) % 2][0], uTs[(sbi + 1) % 2][1])
                    for s in range(6):
                        wt, wb = wts[wi % 2], wbs[wi % 2]
                        wi += 1
                        load_w(wt, wb, wine_d[j2], s * 512, 512)
                        if s == 0 or s == 5:
                            flush(0)
                        tks, tkb = tokst[s % 2], tokstb[s % 2]
                        for tl in range(6):
                            gt = tok0 // 128 + tl
                            isctx = gt >= 16
                            pi = prot.next()
                            P.mm(psF[pi][:], psFb[pi], [(uT[:, c, tl * 128:(tl + 1) * 128], wt[:, c, :]) for c in range(KC)],
                                 [ub, wb])
                            flush(2)
                            if gen is not None and 1 <= s <= 4:
                                next(gen, None)
                            if s == 0 or s == 5:
                                P.op("act", lambda e, pi=pi, tl=tl, tks=tks: e.copy(out=tks[:, tl, :], in_=psF[pi][:]),
                                     reads=[psFb[pi]], writes=[tkb])
                                continue
                            gain = gk if s == 4 else gq
                            sq, sqb, ssq, ssqb, qn, qnb, t1, t1b, t2, t2b = tmps.next()
                            P.op("act", lambda e, pi=pi: e.activation(out=sq[:], in_=psF[pi][:], func=AF.Square),
                                 reads=[psFb[pi]], writes=[sqb])
                            P.op("dve", lambda e: e.tensor_reduce(out=ssq[:], in_=sq[:].rearrange("p (h d) -> p h d", h=4),
                                                                   axis=AX.X, op=ALU.add), reads=[sqb], writes=[ssqb])
                            P.op("act", lambda e: e.activation(out=ssq[:], in_=ssq[:], func=AF.Sqrt, scale=1.0 / 128, bias=epsT[:]),
                                 reads=[ssqb, b_const], writes=[ssqb])
                            P.op("dve", lambda e: e.reciprocal(out=ssq[:], in_=ssq[:]), reads=[ssqb], writes=[ssqb])
                            P.op("dve", lambda e, pi=pi: e.tensor_tensor(
                                out=qn[:].rearrange("p (h d) -> p h d", h=4), in0=psF[pi][:].rearrange("p (h d) -> p h d", h=4),
                                in1=ssq[:].unsqueeze(2).to_broadcast([128, 4, 128]), op=ALU.mult),
                                reads=[psFb[pi], ssqb], writes=[qnb])
                            qt, qtb = qtoks.next()
                            if isctx:
                                P.op("dve", lambda e, gain=gain, qt=qt: e.tensor_tensor(out=qt[:], in0=qn[:], in1=gain[:], op=ALU.mult),
                                     reads=[qnb, gb], writes=[qtb])
                            else:
                                P.op("dve", lambda e, gain=gain: e.tensor_tensor(out=qn[:], in0=qn[:], in1=gain[:], op=ALU.mult),
                                     reads=[qnb, gb], writes=[qnb])
                                rope_tm(qn[:], qnb, tab, tabb, gt, 4, 128, t1, t1b, t2, t2b, qt[:], qtb)
                            def do_tr(qt=qt, qtb=qtb, h0=(s - 1) * 4, tl=tl):
                                ti = trot.next()
                                P.tr(psTb[ti], [(psT[ti][:, h * 128:(h + 1) * 128], qt[:, h * 128:(h + 1) * 128]) for h in range(4)],
                                     [qtb, b_const], identB[:])
                                P.op("act", lambda e: e.copy(out=qkst[:, h0:h0 + 4, tl * 128:(tl + 1) * 128],
                                                             in_=psT[ti][:, 0:512].rearrange("p (h t) -> p h t", h=4)),
                                     reads=[psTb[ti]], writes=[qkstb])

                            pend.append(do_tr)
                        if s == 0:
                            P.dma("sp", F_d[tok0:tok0 + 768, :].rearrange("(t p) n -> p t n", p=128), tks[:], reads=[tkb])
                        if s == 5:
                            P.dma("sp", V_d[tok0:tok0 + 768, 0:512].rearrange("(t p) n -> p t n", p=128), tks[:], reads=[tkb])
                    pend.append(lambda tok0=tok0: P.dma("sp", QKT_d[0:2048, tok0:tok0 + 768].rearrange("(h p) t -> p h t", p=128),
                                                        qkst[:], reads=[qkstb]))
                    if gen is not None:
                        for _ in gen:
                            pass
                flush(0)
                P.barrier()

        def phase_att(l, need_ctx):
            with ExitStack() as ph:
                kT = sb("kTa", [128, 4, NT], BF16, ph)
                vt = sb("vta", [128, NTILE, 512], BF16, ph)
                kvb = Buf()
                P.dma("sp", kT[:], QKT_d[12 * 128:16 * 128, :].rearrange("(g p) t -> p g t", p=128), writes=[kvb])
                P.dma("sp", vt[:], V_d[:, 0:512].rearrange("(t p) n -> p t n", p=128), writes=[kvb])
                qTs = Rot([(sb("qTa%d" % i, [128, NT], BF16, ph), Buf()) for i in range(2)])
                Es = Rot([(sb("Ea%d" % i, [128, 512], BF16, ph), Buf()) for i in range(4)])
                rden = sb("rden", [128, 512], F32, ph)
                rdb = Buf()
                asts = Rot([(sb("ast%d" % i, [128, 512], BF16, ph), Buf()) for i in range(2)])
                srot = Rot([0, 1, 2])
                orot = Rot([((psF[3][:], psFb[3]), (psF[4][:], psFb[4])),
                            ((psF[5][:], psFb[5]), (psT[0][:].bitcast(F32), psTb[0]))])
                bg_only1[0] = True
                qblocks = [(0, 512, 0), (512, 512, 0), (1024, 512, 0), (1536, 512, 0)]
                if need_ctx:
                    qblocks.append((2048, 256, 1))
                scale = 128.0 ** -0.5
                bgw = Rot([(sb("bgwa%d" % i, [128, KC * 256], BF16, ph), Buf()) for i in range(2)])
                for _ in range(2):
                    wt2, wb2 = bgw.next()
                    bg_load(wt2, wb2, l + 2)
                qnext = qTs.next()
                P.dma("sp", qnext[0][:], QKT_d[0:128, :], writes=[qnext[1]])
                for head in range(12):
                    g = head // 3
                    qT, qb = qnext
                    if head + 1 < 12:
                        qnext = qTs.next()
                        P.dma("sp", qnext[0][:], QKT_d[(head + 1) * 128:(head + 2) * 128, :], writes=[qnext[1]])
                    for (q0, qn_, isctx) in qblocks:
                        kts = [16, 17] if isctx else list(range(NTILE))
                        (po_ap, po_b), (pd_ap, pd_b) = orot.next()

                        def s_mm(kt):
                            pi = srot.next()
                            P.mm(psF[pi][:, 0:qn_], psFb[pi], [(kT[:, g, kt * 128:(kt + 1) * 128], qT[:, q0:q0 + qn_])], [kvb, qb])
                            return pi

                        sq_ = [s_mm(kt) for kt in kts[:2]]
                        for ix, kt in enumerate(kts):
                            pi = sq_.pop(0)
                            Et, Eb = Es.next()
                            P.op("act", lambda e: e.activation(out=Et[:, 0:qn_], in_=psF[pi][:, 0:qn_], func=AF.Exp, scale=scale),
                                 reads=[psFb[pi]], writes=[Eb])
                            if ix + 2 < len(kts):
                                sq_.append(s_mm(kts[ix + 2]))
                            first, last = ix == 0, ix == len(kts) - 1
                            P.mm(po_ap[:, 0:qn_], po_b, [(vt[:, kt, g * 128:(g + 1) * 128], Et[:, 0:qn_])], [kvb, Eb],
                                 start=first, stop=last)
                            P.mm(pd_ap[:, 0:qn_], pd_b, [(onesB[:], Et[:, 0:qn_])], [b_const, Eb], start=first, stop=last)
                        P.op("dve", lambda e: e.reciprocal(out=rden[:, 0:qn_], in_=pd_ap[:, 0:qn_]), reads=[pd_b], writes=[rdb])
                        at, ab = asts.next()
                        P.op("dve", lambda e: e.tensor_tensor(out=at[:, 0:qn_], in0=po_ap[:, 0:qn_], in1=rden[:, 0:qn_], op=ALU.mult),
                             reads=[po_b, rdb], writes=[ab])
                        P.dma("sp", YT_d[(4 + head) * 128:(5 + head) * 128, q0:q0 + qn_], at[:, 0:qn_], reads=[ab])
                        nb = 0
                        while bg_compute():
                            nb += 1
                        for _ in range(nb):
                            wt2, wb2 = bgw.next()
                            bg_load(wt2, wb2, l + 2)
                while bg_compute():
                    pass
                bg_only1[0] = False
                P.barrier()

        def phase_fourier(l, need_ctx, inner=None):
            with ExitStack() as ph:
                ft = sb("ftk", [128, NTILE, 512], BF16, ph)
                fb = Buf()
                P.dma("sp", ft[:], F_d.rearrange("(t p) n -> p t n", p=128), writes=[fb])
                cch = sb("cch", [128, 256], BF16, ph)
                cchb = Buf()
                P.dma("pool", cch[:], dftch_d, writes=[cchb])
                cchx = sb("cchx", [128, 256], BF16, ph)
                P.dma("pool", cchx[:], dftchx_d, writes=[cchb])
                slabs = Rot([(sb("dc%d" % i, [128, 16, 512], BF16, ph), sb("ds%d" % i, [128, 16, 512], BF16, ph), Buf())
                             for i in range(2)])
                zs = Rot([(sb("zc%d" % i, [128, 512], BF16, ph), sb("zs%d" % i, [128, 512], BF16, ph), Buf(), Buf())
                          for i in range(2)])
                ysts = Rot([(sb("yst%d" % i, [128, 512], BF16, ph), Buf()) for i in range(2)])
                prot = Rot([(0, 1, 2), (3, 4, 5)])
                blocks = [(0, 512, 0), (512, 512, 0), (1024, 512, 0), (1536, 512, 0)]
                if need_ctx:
                    blocks.append((2048, 256, 1))
                for (t0, n, isctx) in blocks:
                    dc, ds, db = slabs.next()
                    if isctx:
                        nch, tbase = 2, 16
                        P.dma("pool", dc[:, 0:2, 0:256], dftc256_d.rearrange("(c p) n -> p c n", p=128), writes=[db])
                        P.dma("pool", ds[:, 0:2, 0:256], dfts256_d.rearrange("(c p) n -> p c n", p=128), writes=[db])
                    else:
                        nch, tbase = 16, 0
                        P.dma("pool", dc[:], dftc_d[:, t0:t0 + 512].rearrange("(c p) n -> p c n", p=128), writes=[db])
                        P.dma("pool", ds[:], dfts_d[:, t0:t0 + 512].rearrange("(c p) n -> p c n", p=128), writes=[db])
                    for g in range(4):
                        pc, ps_, py = prot.next()
                        zc, zs_, zcb, zsb = zs.next()
                        P.mm(psF[pc][:, 0:n], psFb[pc], [(ft[:, tbase + c, g * 128:(g + 1) * 128], dc[:, c, 0:n]) for c in range(nch)],
                             [fb, db])
                        P.mm(psF[ps_][:, 0:n], psFb[ps_], [(ft[:, tbase + c, g * 128:(g + 1) * 128], ds[:, c, 0:n]) for c in range(nch)],
                             [fb, db])
                        P.op("act", lambda e, pc=pc, zc=zc: e.copy(out=zc[:, 0:n], in_=psF[pc][:, 0:n]), reads=[psFb[pc]], writes=[zcb])
                        P.op("dve", lambda e, ps_=ps_, zs_=zs_: e.tensor_copy(out=zs_[:, 0:n], in_=psF[ps_][:, 0:n]), reads=[psFb[ps_]],
                             writes=[zsb])
                        P.mm(psF[py][:, 0:n], psFb[py], [((cchx if isctx else cch)[:, 0:128], zc[:, 0:n]), ((cchx if isctx else cch)[:, 128:256], zs_[:, 0:n])],
                             [cchb, zcb, zsb])
                        yt_, ytb = ysts.next()
                        P.op("act", lambda e, py=py, yt_=yt_: e.copy(out=yt_[:, 0:n], in_=psF[py][:, 0:n]), reads=[psFb[py]],
                             writes=[ytb])
                        P.dma("sp", YT_d[g * 128:(g + 1) * 128, t0:t0 + n], yt_[:, 0:n], reads=[ytb])
                if inner is not None:
                    inner()
                else:
                    P.barrier()

        def phase_proj_odd(l, j2, need_ctx):
            with ExitStack() as ph:
                tab = sb("ropeO", [128, 16, 512], F32, ph)
                tabb = Buf()
                P.dma("sp", tab[:], ropeO_d.rearrange("(t p) n -> p t n", p=128), writes=[tabb])
                uTs = [(sb("uTo%d" % i, [128, KC, 768], BF16, ph), Buf()) for i in range(2)]
                mr = ModRes(ph, "o", 2)
                wts = [sb("wod%d" % i, [128, KC, 512], BF16, ph) for i in range(2)]
                wbs = [Buf() for _ in range(2)]
                qkst = [sb("qksto%d" % i, [128, 4, 768], BF16, ph) for i in range(2)]
                qkstb = [Buf() for _ in range(2)]
                tokst = [sb("toksto%d" % i, [128, 6, 512], BF16, ph) for i in range(2)]
                tokstb = [Buf() for _ in range(2)]
                tmps = Rot([(sb("t1o%d" % i, [128, 512], F32, ph), Buf(), sb("t2o%d" % i, [128, 512], F32, ph), Buf())
                            for i in range(3)])
                prot = Rot(list(range(4)))
                trot = Rot([0, 1])
                wi = 0
                pend = []

                def flush(keep):
                    while len(pend) > keep:
                        pend.pop(0)()

                modulate_block(mr, hA, 0, l, 0, 1, uTs[0][0], uTs[0][1])
                for sbi, (tok0, pieces) in enumerate(SBLOCKS):
                    uT, ub = uTs[sbi % 2]
                    gen = None
                    if sbi + 1 < len(SBLOCKS):
                        gen = modulate_gen(mr, hA, sbi + 1, l, 0, 1, uTs[(sbi + 1) % 2][0], uTs[(sbi + 1) % 2][1])
                    for s in range(24):
                        wt, wb = wts[wi % 2], wbs[wi % 2]
                        wi += 1
                        load_w(wt, wb, wino_d[j2], s * 512, 512)
                        if s >= 8 or s == 0:
                            flush(0)
                        tks, tkb = tokst[s % 2], tokstb[s % 2]
                        qks, qkb = qkst[s % 2], qkstb[s % 2]
                        for tl in range(6):
                            gt = tok0 // 128 + tl
                            isctx = gt >= 16
                            pi = prot.next()
                            P.mm(psF[pi][:], psFb[pi], [(uT[:, c, tl * 128:(tl + 1) * 128], wt[:, c, :]) for c in range(KC)],
                                 [ub, wb])
                            flush(2)
                            if gen is not None and s < 8 and tl % 2 == 0:
                                next(gen, None)
                            if s >= 16:
                                P.op("act", lambda e, pi=pi, tl=tl, tks=tks: e.activation(out=tks[:, tl, :], in_=psF[pi][:], func=AF.Silu),
                                     reads=[psFb[pi]], writes=[tkb])
                                continue
                            if s >= 8:
                                P.op("act", lambda e, pi=pi, tl=tl, tks=tks: e.copy(out=tks[:, tl, :], in_=psF[pi][:]),
                                     reads=[psFb[pi]], writes=[tkb])
                                continue
                            if isctx:
                                P.op("act", lambda e, pi=pi, tl=tl, tks=tks: e.copy(out=tks[:, tl, :], in_=psF[pi][:]),
                                     reads=[psFb[pi]], writes=[tkb])
                            else:
                                t1, t1b, t2, t2b = tmps.next()
                                rope_tm(psF[pi][:], psFb[pi], tab, tabb, gt, 2, 256, t1, t1b, t2, t2b, tks[:, tl, :], tkb)
                            def do_tr(tks=tks, tkb=tkb, qks=qks, qkb=qkb, tl=tl):
                                ti = trot.next()
                                P.tr(psTb[ti], [(psT[ti][:, h * 128:(h + 1) * 128], tks[:, tl, h * 128:(h + 1) * 128])
                                                for h in range(4)], [tkb, b_const], identB[:])
                                P.op("act", lambda e: e.copy(out=qks[:, :, tl * 128:(tl + 1) * 128],
                                                             in_=psT[ti][:, 0:512].rearrange("p (h t) -> p h t", h=4)),
                                     reads=[psTb[ti]], writes=[qkb])

                            pend.append(do_tr)
                        if s < 8:
                            pend.append(lambda s=s, tok0=tok0, qks=qks, qkb=qkb: P.dma(
                                "sp", QKT_d[s * 512:(s + 1) * 512, tok0:tok0 + 768].rearrange("(h p) t -> p h t", p=128), qks[:],
                                reads=[qkb]))
                        if 4 <= s < 8:
                            pend.append(lambda s=s, tok0=tok0, tks=tks, tkb=tkb: P.dma(
                                "sp", KTOK_d[tok0:tok0 + 768, (s - 4) * 512:(s - 3) * 512].rearrange("(t p) n -> p t n", p=128),
                                tks[:], reads=[tkb]))
                        if 8 <= s < 16:
                            P.dma("sp", V_d[tok0:tok0 + 768, (s - 8) * 512:(s - 7) * 512].rearrange("(t p) n -> p t n", p=128),
                                  tks[:], reads=[tkb])
                        if s >= 16:
                            P.dma("sp", SG_d[tok0:tok0 + 768, (s - 16) * 512:(s - 15) * 512].rearrange("(t p) n -> p t n", p=128),
                                  tks[:], reads=[tkb])
                    if gen is not None:
                        for _ in gen:
                            pass
                flush(0)
                P.barrier()

        def phase_ret(l, j2, need_ctx):
            with ExitStack() as ph:
                rett = sb("rett", [128, 6, 128], F32, ph)
                retc = sb("retc", [128, 2], F32, ph)
                lg = sb("lg", [128, 2, 8], F32, ph)
                cb = Buf()
                P.dma("sp", rett[:], rett_d, writes=[cb])
                P.dma("sp", retc[:], retc_d, writes=[cb])
                P.dma("sp", lg[:, 0, :], lgf_d[j2], writes=[cb])
                P.dma("sp", lg[:, 1, :], lgb_d[j2], writes=[cb])
                Mall = sb("Mall", [128, 8, 128], F32, ph)
                qdec = sb("qdec", [128, 2, 8, 128], F32, ph)
                kdec = sb("kdec", [128, 2, 8], F32, ph)
                cdec = sb("cdec", [128, 2, 8], F32, ph)
                e1 = sb("e1", [128, 128], F32, ph)
                e1b = Buf()
                tb = Buf()
                for h in range(8):
                    for d_ in range(2):
                        P.op("act", lambda e, h=h, d_=d_: e.activation(out=e1[:], in_=rett[:, 2 * d_, :], func=AF.Exp,
                                                                       scale=lg[:, d_, h:h + 1]), reads=[cb], writes=[e1b])
                        if d_ == 0:
                            P.op("dve", lambda e, h=h: e.tensor_tensor(out=Mall[:, h, :], in0=e1[:], in1=rett[:, 1, :], op=ALU.mult),
                                 reads=[e1b, cb], writes=[tb])
                        else:
                            P.op("dve", lambda e: e.tensor_tensor(out=e1[:], in0=e1[:], in1=rett[:, 3, :], op=ALU.mult),
                                 reads=[e1b, cb], writes=[e1b])
                            P.op("dve", lambda e, h=h: e.tensor_tensor(out=Mall[:, h, :], in0=Mall[:, h, :], in1=e1[:], op=ALU.add),
                                 reads=[e1b, tb], writes=[tb])
                        P.op("act", lambda e, h=h, d_=d_: e.activation(out=qdec[:, d_, h, :], in_=rett[:, 4 + d_, :], func=AF.Exp,
                                                                       scale=lg[:, d_, h:h + 1]), reads=[cb], writes=[tb])
                        P.op("act", lambda e, h=h, d_=d_: e.activation(out=kdec[:, d_, h:h + 1], in_=retc[:, d_:d_ + 1], func=AF.Exp,
                                                                       scale=lg[:, d_, h:h + 1]), reads=[cb], writes=[tb])
                        P.op("act", lambda e, h=h, d_=d_: e.activation(out=cdec[:, d_, h:h + 1], in_=lg[:, d_, h:h + 1], func=AF.Exp,
                                                                       scale=128.0), reads=[cb], writes=[tb])
                P.op("dve", lambda e: e.tensor_scalar(out=kdec[:], in0=kdec[:], scalar1=1.0 / 16, scalar2=None, op0=ALU.mult),
                     reads=[tb], writes=[tb])
                qT = sb("qTr", [128, 2, NT], BF16, ph)
                kT = sb("kTr", [128, 2, NT], BF16, ph)
                ktk = sb("ktkr", [128, NTILE, 256], BF16, ph)
                rawb = Buf()
                vtks = Rot([(sb("vtkr%d" % i, [128, NTILE, 512], BF16, ph), Buf()) for i in range(2)])
                sgt = sb("sgr", [128, NTILE, 512], BF16, ph)
                sgb = Buf()
                part = sb("partr", [128, NTILE, 512], F32, ph)
                partb = [Buf() for _ in range(NTILE)]
                S = [sb("Sst%d" % i, [128, 2, 512], F32, ph) for i in range(2)]
                Sb_ = [[Buf(), Buf()], [Buf(), Buf()]]
                Sbf = [Rot([(sb("Sbf%d_%d" % (d_, i), [128, 2, 512], BF16, ph), Buf()) for i in range(3)]) for d_ in range(2)]
                pTs = Rot([(sb("pTr%d" % i, [128, 128], BF16, ph), Buf()) for i in range(2)])
                ksr = [Rot([(sb("ksr%d_%d" % (d_, i), [128, 256], BF16, ph), Buf()) for i in range(3)]) for d_ in range(2)]
                qsr = [Rot([(sb("qsr%d_%d" % (d_, i), [128, 2, 128], BF16, ph), Buf()) for i in range(3)]) for d_ in range(2)]
                ots = Rot([(sb("otr%d" % i, [128, 512], F32, ph), Buf()) for i in range(7)])
                junks = Rot([(sb("junkr%d" % i, [128, 512], F32, ph), Buf()) for i in range(4)])
                ssqs = Rot([(sb("ssqr%d" % i, [128, 1], F32, ph), Buf()) for i in range(6)])
                yts = Rot([(sb("ytr%d" % i, [128, 512], BF16, ph), Buf()) for i in range(4)])
                gstep = [0]
                stq = []

                def sched(fn):
                    stq.append((gstep[0] + 1, fn))

                def run_due():
                    while stq and stq[0][0] <= gstep[0]:
                        stq.pop(0)[1]()
                ystg = Rot([(sb("ystg%d" % i, [128, 4, 128], BF16, ph), Buf()) for i in range(2)])
                sc_b = [psFb[0], psFb[0]]
                sc_aps = [psF[0][:, 0:128], psF[0][:, 0:128]]
                psU = [[(psF[2][:], psFb[2]), (psF[3][:], psFb[3])], [(psF[5][:], psFb[5]), (psT[1][:].bitcast(F32), psTb[1])]]
                psO = [(psF[1][:], psFb[1]), (psF[4][:], psFb[4])]
                tr_b = [psTb[0], psTb[0]]
                tr_aps = [psT[0][:, 0:512], psT[0][:, 0:512]]
                trot = Rot([0, 1])
                fo = [16, 17] + list(range(16))
                bo = [17, 16] + list(range(15, -1, -1))
                kf = {c: i for i, c in enumerate(fo)}
                kb = {c: i for i, c in enumerate(bo)}
                def load_raw(h):
                    P.dma("sp", qT[:], QKT_d[h * 256:(h + 1) * 256, :].rearrange("(c p) t -> p c t", p=128), writes=[rawb])
                    P.dma("sp", kT[:], QKT_d[2048 + h * 256:2048 + (h + 1) * 256, :].rearrange("(c p) t -> p c t", p=128),
                          writes=[rawb])
                    P.dma("sp", ktk[:], KTOK_d[:, h * 256:(h + 1) * 256].rearrange("(t p) n -> p t n", p=128), writes=[rawb])

                def load_sg(h):
                    P.dma("sp", sgt[:], SG_d[:, h * 512:(h + 1) * 512].rearrange("(t p) n -> p t n", p=128), writes=[sgb])

                def load_v(h):
                    vt_, vb_ = vtks.next()
                    P.dma("sp", vt_[:], V_d[:, h * 512:(h + 1) * 512].rearrange("(t p) n -> p t n", p=128), writes=[vb_])
                    return vt_, vb_

                load_raw(0)
                vnext = load_v(0)
                for h in range(8):
                    vtk, vb = vnext
                    want = lambda c: need_ctx or c < 16

                    def scale_ops(k):
                        r = {}
                        for d_, c in ((0, fo[k]), (1, bo[k])):
                            cs = slice(c * 128, (c + 1) * 128)
                            if k < NTILE - 1:
                                ks, ksb = ksr[d_].next()
                                P.op("pool", lambda e: e.tensor_scalar(out=ks[:], in0=ktk[:, c, :], scalar1=kdec[:, d_, h:h + 1],
                                                                        scalar2=0.0, op0=ALU.mult, op1=ALU.add),
                                     reads=[rawb, tb], writes=[ksb])
                                r["ks%d" % d_] = (ks, ksb)
                            if k > 0 and want(c):
                                qs, qsb = qsr[d_].next()
                                P.op("pool", lambda e: e.tensor_tensor(
                                    out=qs[:], in0=qT[:, :, cs], in1=qdec[:, d_, h, :].unsqueeze(1).to_broadcast([128, 2, 128]),
                                    op=ALU.mult), reads=[rawb, tb], writes=[qsb])
                                r["qs%d" % d_] = (qs, qsb)
                        return r

                    def finalize(ot, otb, c, h=h):
                        jk, jkb = junks.next()
                        P.op("act", lambda e: e.activation(out=jk[:], in_=ot[:], func=AF.Square), reads=[otb], writes=[jkb])

                        def st2():
                            ssq, ssqb = ssqs.next()
                            P.op("dve", lambda e: e.tensor_reduce(out=ssq[:], in_=jk[:], axis=AX.X, op=ALU.add), reads=[jkb],
                                 writes=[ssqb])
                            P.op("act", lambda e: e.activation(out=ssq[:], in_=ssq[:], func=AF.Sqrt, scale=1.0 / 512, bias=epsT[:]),
                                 reads=[ssqb, b_const], writes=[ssqb])

                            def st3():
                                P.op("dve", lambda e: e.reciprocal(out=ssq[:], in_=ssq[:]), reads=[ssqb], writes=[ssqb])
                                yt_, ytb = yts.next()
                                P.op("dve", lambda e: e.scalar_tensor_tensor(out=yt_[:], in0=ot[:], scalar=ssq[:, 0:1],
                                                                              in1=sgt[:, c, :], op0=ALU.mult, op1=ALU.mult),
                                     reads=[otb, ssqb, sgb], writes=[ytb])

                                def st4():
                                    ti = trot.next()
                                    P.tr(tr_b[ti], [(tr_aps[ti][:, e_ * 128:(e_ + 1) * 128], yt_[:, e_ * 128:(e_ + 1) * 128])
                                                    for e_ in range(4)], [ytb, b_const], identB[:])
                                    yg, ygb = ystg.next()
                                    P.op("act", lambda e: e.copy(out=yg[:], in_=tr_aps[ti].rearrange("p (h t) -> p h t", h=4)),
                                         reads=[tr_b[ti]], writes=[ygb])
                                    P.dma("sp", YT_d[h * 512:(h + 1) * 512, c * 128:(c + 1) * 128].rearrange("(e p) t -> p e t", p=128),
                                          yg[:], reads=[ygb])

                                sched(st4)

                            sched(st3)

                        sched(st2)

                    def deliver(c, ps_ap, ps_buf, dirn):
                        first = (kf[c] < kb[c]) if dirn == 0 else (kb[c] < kf[c])
                        if first:
                            P.op("act", lambda e: e.copy(out=part[:, c, :], in_=ps_ap), reads=[ps_buf], writes=[partb[c]])
                            return
                        ot, otb = ots.next()
                        other_absent = (dirn == 0 and kb[c] == 0)
                        if other_absent:
                            P.op("dve", lambda e: e.tensor_copy(out=ot[:], in_=ps_ap), reads=[ps_buf], writes=[otb])
                        else:
                            P.op("dve", lambda e: e.tensor_tensor(out=ot[:], in0=part[:, c, :], in1=ps_ap, op=ALU.add),
                                 reads=[ps_buf, partb[c]], writes=[otb])
                        finalize(ot, otb, c)

                    cur = [None, None]
                    sc = {0: scale_ops(0)}
                    for k in range(NTILE):
                        gstep[0] += 1
                        if k == 2 or (h == 0 and k == 0):
                            if not (h == 0 and k == 2):
                                load_sg(h)
                        if k + 1 < NTILE:
                            sc[k + 1] = scale_ops(k + 1)
                        r = sc.pop(k)
                        cf, cb_ = fo[k], bo[k]
                        csf = slice(cf * 128, (cf + 1) * 128)
                        if want(cf):
                            P.mm(sc_aps[k % 2], sc_b[k % 2], [(kT[:, dc, csf], qT[:, dc, csf]) for dc in range(2)], [rawb])
                        if k < NTILE - 1:
                            for d_, c in ((0, cf), (1, cb_)):
                                ks, ksb = r["ks%d" % d_]
                                for dc in range(2):
                                    ua, ub_ = psU[d_][dc]
                                    P.mm(ua, ub_, [(ks[:, dc * 128:(dc + 1) * 128], vtk[:, c, :])], [ksb, vb])
                        prev = [cur[0], cur[1]]
                        if want(cf):
                            pT, pTb = pTs.next()
                            P.op("dve", lambda e: e.tensor_tensor(out=pT[:], in0=sc_aps[k % 2], in1=Mall[:, h, :], op=ALU.mult),
                                 reads=[sc_b[k % 2], tb], writes=[pTb])
                        if k < NTILE - 1:
                            for d_ in range(2):
                                nxt = Sbf[d_].next()
                                for dc in range(2):
                                    ua, ub_ = psU[d_][dc]
                                    if k == 0:
                                        P.op("act", lambda e: e.copy(out=S[d_][:, dc, :], in_=ua), reads=[ub_], writes=[Sb_[d_][dc]])
                                    else:
                                        P.op("dve", lambda e: e.scalar_tensor_tensor(
                                            out=S[d_][:, dc, :], in0=S[d_][:, dc, :], scalar=cdec[:, d_, h:h + 1], in1=ua,
                                            op0=ALU.mult, op1=ALU.add), reads=[ub_, Sb_[d_][dc], tb], writes=[Sb_[d_][dc]])
                                    P.op("act", lambda e: e.copy(out=nxt[0][:, dc, :], in_=S[d_][:, dc, :]), reads=[Sb_[d_][dc]],
                                         writes=[nxt[1]])
                                cur[d_] = nxt
                        run_due()
                        if want(cf):
                            items = [(pT[:], vtk[:, cf, :])]
                            rds = [pTb, vb]
                            if k > 0:
                                qs, qsb = r["qs0"]
                                items += [(qs[:, dc, :], prev[0][0][:, dc, :]) for dc in range(2)]
                                rds += [qsb, prev[0][1]]
                            P.mm(psO[0][0], psO[0][1], items, rds)
                            deliver(cf, psO[0][0], psO[0][1], 0)
                        if want(cb_) and k > 0:
                            qs, qsb = r["qs1"]
                            P.mm(psO[1][0], psO[1][1], [(qs[:, dc, :], prev[1][0][:, dc, :]) for dc in range(2)], [qsb, prev[1][1]])
                            deliver(cb_, psO[1][0], psO[1][1], 1)
                        if k == 11 and h + 1 < 8:
                            vnext = load_v(h + 1)
                    if h + 1 < 8:
                        load_raw(h + 1)
                for _ in range(4):
                    gstep[0] += 1
                    run_due()
                assert not stq
                P.barrier()

        def phase_final(hsrc):
            with ExitStack() as ph:
                hin = [sb("fin%d" % i, [128, KC, 128], F32, ph) for i in range(2)]
                hinb = [Buf() for _ in range(2)]
                stg = [sb("fstg%d" % i, [128, D], F32, ph) for i in range(2)]
                stgb = [Buf() for _ in range(2)]
                prot = Rot(list(range(4)))
                for tt in range(16):
                    hi, hb = hin[tt % 2], hinb[tt % 2]
                    sg_, sgb = stg[tt % 2], stgb[tt % 2]
                    P.dma("sp", hi[:], hsrc[:, tt * 128:(tt + 1) * 128].rearrange("(c p) t -> p c t", p=128), writes=[hb])
                    for cg in range(4):
                        pi = prot.next()
                        P.tr(psFb[pi], [(psF[pi][:, i * 128:(i + 1) * 128], hi[:, cg * 4 + i, :]) for i in range(4)], [hb, b_const],
                             identF[:])
                        if cg % 2 == 0:
                            P.op("act", lambda e, pi=pi, cg=cg, sg_=sg_: e.copy(out=sg_[:, cg * 512:(cg + 1) * 512], in_=psF[pi][:]),
                                 reads=[psFb[pi]], writes=[sgb])
                        else:
                            P.op("dve", lambda e, pi=pi, cg=cg, sg_=sg_: e.tensor_copy(out=sg_[:, cg * 512:(cg + 1) * 512], in_=psF[pi][:]),
                                 reads=[psFb[pi]], writes=[sgb])
                    P.dma("pool", y_d[tt * 128:(tt + 1) * 128, :], sg_[:], reads=[sgb])
                P.barrier()

        def dump(hsrc):
            with ExitStack() as ph:
                t = sb("dmp", [128, KC, 576], F32, ph)
                tb_ = Buf()
                for i in range(4):
                    P.dma("sp", t[:], hsrc[:, i * 576:(i + 1) * 576].rearrange("(c p) t -> p c t", p=128), writes=[tb_])
                    P.dma("sp", dbg_d[:, i * 576:(i + 1) * 576].rearrange("(c p) t -> p c t", p=128), t[:], reads=[tb_])
                P.barrier()

        def program():
            if done("init"):
                phase_init()
                return hA
            phase_init(phase_mod)
            for l in range(DEPTH):
                need_ctx = l < DEPTH - 1
                j2 = l // 2
                if l % 2 == 0:
                    phase_proj_even(l, j2, need_ctx)
                    phase_fourier(l, need_ctx, lambda l=l, need_ctx=need_ctx: phase_att(l, need_ctx))
                    phase_out(l, woute_d[j2], need_ctx)
                else:
                    phase_proj_odd(l, j2, need_ctx)
                    phase_ret(l, j2, need_ctx)
                    phase_out(l, wouto_d[j2], need_ctx)
                if done("mix%d" % l):
                    return hB
                phase_ffn(l, need_ctx)
                if done("ffn%d" % l):
                    return hA
            return hA

        hfin = program()
        if dump_h:
            dump(hfin)
        phase_final(hfin)
        print("instructions:", P.nins, {k: e.cnt for k, e in P.E.items()})
    return nc


def _rope_table(hd_axis):
    nf = hd_axis // 2
    inv = (10000.0 ** (-np.arange(0, hd_axis, 2, dtype=np.float32) / hd_axis)).astype(np.float32)
    t = np.arange(NL)
    row = (t // 64).astype(np.float32)
    col = (t % 64).astype(np.float32)
    cos_parts, sin_parts = [], []
    for pos in (row, col):
        ang = (pos[:, None] * inv[None, :]).astype(np.float32)
        c, s = np.cos(ang).astype(np.float32), np.sin(ang).astype(np.float32)
        cos_parts += [c, c]
        sin_parts += [-s, s]
    return np.ascontiguousarray(np.concatenate(cos_parts + sin_parts, axis=1).astype(np.float32))


def _consts():
    c = {}
    c["ropeE"] = _rope_table(64)
    c["ropeO"] = _rope_table(128)
    t = np.arange(NL, dtype=np.int64)
    m = (t[:, None] * t[None, :]) % NL
    ang = 2 * np.pi * m / NL
    c["dftc"] = np.cos(ang).astype(np.float32)
    c["dfts"] = np.sin(ang).astype(np.float32)
    t = np.arange(NX, dtype=np.int64)
    ang = 2 * np.pi * ((t[:, None] * t[None, :]) % NX) / NX
    c["dftc256"] = np.cos(ang).astype(np.float32)
    c["dfts256"] = np.sin(ang).astype(np.float32)
    t = np.arange(128, dtype=np.int64)
    ang = 2 * np.pi * ((t[:, None] * t[None, :]) % 128) / 128
    ch = np.concatenate([np.cos(ang), -np.sin(ang)], axis=1)
    c["dftch"] = (ch / np.sqrt(NL * 128.0)).astype(np.float32)
    c["dftchx"] = (ch / np.sqrt(NX * 128.0)).astype(np.float32)
    j = np.arange(128, dtype=np.float32)[:, None]
    i = np.arange(128, dtype=np.float32)[None, :]
    rett = np.zeros((128, 6, 128), np.float32)
    rett[:, 0] = np.maximum(i - j, 0)
    rett[:, 1] = (i >= j) / 16.0
    rett[:, 2] = np.maximum(j - i, 0)
    rett[:, 3] = (j > i) / 16.0
    rett[:, 4] = np.broadcast_to(i + 1, (128, 128))
    rett[:, 5] = np.broadcast_to(128 - i, (128, 128))
    c["rett"] = rett
    retc = np.zeros((128, 2), np.float32)
    retc[:, 0] = 127 - np.arange(128)
    retc[:, 1] = np.arange(128)
    c["retc"] = retc
    return c


_NC_CACHE = {}


def _prep_inputs(inp):
    f = lambda a: np.ascontiguousarray(np.asarray(a, dtype=np.float32))
    shared = dict(_consts())
    shared["w_mod"] = f(inp["w_mod"])
    shared["bmodT"] = f(np.asarray(inp["b_mod"]).reshape(DEPTH, 96, 128).transpose(2, 0, 1))
    shared["w_in_even"] = f(inp["w_in_even"])
    shared["w_out_even"] = f(inp["w_out_even"])
    shared["qg_rep"] = f(np.tile(np.asarray(inp["q_gain_even"])[:, None, :], (1, 128, 4)))
    shared["kg_rep"] = f(np.tile(np.asarray(inp["k_gain_even"])[:, None, :], (1, 128, 4)))
    shared["w_in_odd"] = f(inp["w_in_odd"])
    shared["w_out_odd"] = f(inp["w_out_odd"])
    shared["lgf_rep"] = f(np.tile(np.asarray(inp["log_decay_fwd"])[:, None, :], (1, 128, 1)))
    shared["lgb_rep"] = f(np.tile(np.asarray(inp["log_decay_bwd"])[:, None, :], (1, 128, 1)))
    shared["w_ffn_in"] = f(inp["w_ffn_in"])
    shared["w_ffn_out"] = f(inp["w_ffn_out"])
    shared["cxT"] = f(np.asarray(inp["c_ctx"]).reshape(KC, 128).T)
    x = np.asarray(inp["x"])
    c = np.asarray(inp["c"])
    ctx = np.asarray(inp["ctx"])
    maps = []
    for b in range(NCORES):
        m = dict(shared)
        m["x"] = f(x[b])
        m["ctx"] = f(ctx[b])
        m["cT"] = f(c[b].reshape(KC, 128).T)
        maps.append(m)
    return maps


def kernel(**inputs):
    if "nc" not in _NC_CACHE:
        _NC_CACHE["nc"] = build()
    nc = _NC_CACHE["nc"]
    maps = _prep_inputs(inputs)
    res = run_bass_kernel_spmd(nc, maps, core_ids=list(range(NCORES)))
    return np.stack([np.asarray(r["y"], dtype=np.float32) for r in res.results], axis=0)
```
